# Optimizing a Trainium2 kernel written in Bass

```python
import math
import jax
import jax.numpy as jnp
from jax import lax
import numpy as np


D_MODEL = 1024
BATCH = 8
SEQ = 4096
DEPTH = 1

SSM_WIDTH = 512
SSM_GROUP = 16
SSM_GROUPS = SSM_WIDTH // SSM_GROUP
SSM_STATE = 64
NSA_HEADS = 8
NSA_KV_HEADS = 2
NSA_REP = NSA_HEADS // NSA_KV_HEADS
HEAD_DIM = 64
NSA_WIDTH = NSA_HEADS * HEAD_DIM
KV_WIDTH = NSA_KV_HEADS * HEAD_DIM
CMP_BLOCK = 32
CMP_STRIDE = 16
CMP_HIDDEN = 128
SEL_BLOCK = 64
N_SELECT = 16
WINDOW = 512
Q_BLOCK = 128
N_BRANCH = 2
D_FF = 2816
CONV_WIDTH = 3
PLE_DIM = 256
EPS = 1e-6
NEG = -1e30
FORCE_BONUS = 100.0

OFF_SSM = 0
OFF_Q = OFF_SSM + SSM_WIDTH
OFF_KV = OFF_Q + NSA_WIDTH
OFF_NSA_GATE = OFF_KV + 6 * KV_WIDTH
OFF_MERGE = OFF_NSA_GATE + 3 * NSA_HEADS
IN_WIDTH = OFF_MERGE + N_BRANCH * D_MODEL

kernel_name = 'hybrid_s5_nsa_convffn_ple'


def rmsnorm(x, g):
    xf = x.astype(jnp.float32)
    y = xf * lax.rsqrt(jnp.mean(xf * xf, axis=-1, keepdims=True) + EPS)
    return (y * g.astype(jnp.float32)).astype(x.dtype)


def s5_mixer(u, a_re, a_im, log_dt, b_re, b_im, c_re, c_im, d_skip, w_glu):
    f32 = jnp.float32
    bsz, t, _ = u.shape
    uf = u.astype(f32)
    ug = uf.reshape(bsz, t, SSM_GROUPS, SSM_GROUP)
    lam = lax.complex(jnp.minimum(a_re.astype(f32), -1e-4), a_im.astype(f32))
    dt = jnp.exp(log_dt.astype(f32))[:, None]
    a_bar = jnp.exp(lam * dt)
    b_bar = lax.complex(b_re.astype(f32), b_im.astype(f32)) * ((a_bar - 1.0) / lam)[:, :, None]
    bu = lax.complex(jnp.einsum('btgc,gpc->btgp', ug, jnp.real(b_bar)),
                     jnp.einsum('btgc,gpc->btgp', ug, jnp.imag(b_bar)))
    a_seq = jnp.broadcast_to(a_bar, bu.shape)

    def combine(left, right):
        a_l, b_l = left
        a_r, b_r = right
        return a_r * a_l, a_r * b_l + b_r

    _, states = lax.associative_scan(combine, (a_seq, bu), axis=1)
    y = (jnp.einsum('gcp,btgp->btgc', c_re.astype(f32), jnp.real(states))
         - jnp.einsum('gcp,btgp->btgc', c_im.astype(f32), jnp.imag(states)))
    y = y.reshape(bsz, t, SSM_WIDTH) + d_skip.astype(f32) * uf
    y = jax.nn.gelu(y)
    y = y * jax.nn.sigmoid(y @ w_glu.astype(f32))
    return y.astype(u.dtype)


def compress_blocks(k, pe, w1, w2):
    bsz, g, t, dh = k.shape
    ch = k.reshape(bsz, g, t // CMP_STRIDE, CMP_STRIDE, dh)
    blocks = jnp.concatenate([ch[:, :, :-1], ch[:, :, 1:]], axis=3) + pe
    flat = blocks.reshape(bsz, g, blocks.shape[2], CMP_BLOCK * dh)
    return jax.nn.gelu(flat @ w1) @ w2


def nsa_mixer(q, kv, gate_logits, pe_k, pe_v, wk1, wk2, wv1, wv2):
    f32 = jnp.float32
    bsz, t, _ = q.shape
    G, R, dh = NSA_KV_HEADS, NSA_REP, HEAD_DIM
    nb = t // Q_BLOCK
    n_cmp = t // CMP_STRIDE - 1
    n_slc = t // SEL_BLOCK
    n_sel = min(N_SELECT, n_slc)
    qh = (q * (dh ** -0.5)).reshape(bsz, t, G, R, dh).transpose(0, 2, 3, 1, 4)
    kvh = kv.reshape(bsz, t, 6, G, dh).transpose(2, 0, 3, 1, 4)
    k_cmp, v_cmp, k_sel, v_sel, k_win, v_win = kvh[0], kvh[1], kvh[2], kvh[3], kvh[4], kvh[5]
    pos = jnp.arange(t)

    kc = compress_blocks(k_cmp, pe_k, wk1, wk2)
    vc = compress_blocks(v_cmp, pe_v, wv1, wv2)
    cmp_end = jnp.arange(n_cmp) * CMP_STRIDE + CMP_BLOCK - 1
    cmask = cmp_end[None, :] <= pos[:, None]
    s = jnp.einsum('bgrtd,bgnd->bgrtn', qh, kc).astype(f32)
    p_cmp = jnp.where(cmask, jax.nn.softmax(jnp.where(cmask, s, NEG), axis=-1), 0.0)
    o_cmp = jnp.einsum('bgrtn,bgnd->bgrtd', p_cmp.astype(vc.dtype), vc)

    starts = np.arange(n_cmp) * CMP_STRIDE
    sel_starts = np.arange(n_slc) * SEL_BLOCK
    overlap = jnp.asarray(((starts[:, None] < sel_starts[None, :] + SEL_BLOCK)
                           & (starts[:, None] + CMP_BLOCK > sel_starts[None, :])).astype(np.float32))
    imp = jnp.einsum('bgrtn,nj->bgtj', p_cmp, overlap)
    cur = pos // SEL_BLOCK
    blk = jnp.arange(n_slc)
    valid = blk[None, :] <= cur[:, None]
    forced = (blk[None, :] == 0) | (blk[None, :] == cur[:, None]) | (blk[None, :] == cur[:, None] - 1)
    score = jnp.where(valid, imp, -1.0) + jnp.where(forced, FORCE_BONUS, 0.0)
    _, sel_idx = lax.top_k(score, n_sel)

    n_keys = n_sel * SEL_BLOCK
    bi = jnp.arange(bsz)[:, None, None]
    gi = jnp.arange(G)[None, :, None]
    offs = jnp.arange(SEL_BLOCK)

    def sel_block(args):
        qb, ib, tb = args
        tok = (ib[..., None] * SEL_BLOCK + offs).reshape(bsz, G, Q_BLOCK * n_keys)
        kg = k_sel[bi, gi, tok].reshape(bsz, G, Q_BLOCK, n_keys, dh)
        vg = v_sel[bi, gi, tok].reshape(bsz, G, Q_BLOCK, n_keys, dh)
        m = tok.reshape(bsz, G, Q_BLOCK, n_keys) <= tb[None, None, :, None]
        sb = jnp.einsum('bgrqd,bgqkd->bgrqk', qb, kg).astype(f32)
        pb = jax.nn.softmax(jnp.where(m[:, :, None], sb, NEG), axis=-1)
        return jnp.einsum('bgrqk,bgqkd->bgrqd', pb.astype(vg.dtype), vg)

    q_blocks = qh.reshape(bsz, G, R, nb, Q_BLOCK, dh).transpose(3, 0, 1, 2, 4, 5)
    i_blocks = sel_idx.reshape(bsz, G, nb, Q_BLOCK, n_sel).transpose(2, 0, 1, 3, 4)
    t_blocks = pos.reshape(nb, Q_BLOCK)
    o_sel = lax.map(sel_block, (q_blocks, i_blocks, t_blocks))
    o_sel = o_sel.transpose(1, 2, 3, 0, 4, 5).reshape(bsz, G, R, t, dh)

    nw = WINDOW // Q_BLOCK

    def band(z):
        zb = jnp.pad(z.reshape(bsz, G, nb, Q_BLOCK, dh), ((0, 0), (0, 0), (nw, 0), (0, 0), (0, 0)))
        return jnp.concatenate([zb[:, :, j:j + nb] for j in range(nw + 1)], axis=3)

    kw = band(k_win)
    vw = band(v_win)
    kpos = ((jnp.arange(nb)[:, None] - nw) * Q_BLOCK + jnp.arange((nw + 1) * Q_BLOCK)[None, :])[:, None, :]
    qpos = t_blocks[:, :, None]
    wmask = (kpos >= 0) & (kpos <= qpos) & (qpos - kpos < WINDOW)
    qw = qh.reshape(bsz, G, R, nb, Q_BLOCK, dh)
    sw = jnp.einsum('bgrnqd,bgnkd->bgrnqk', qw, kw).astype(f32)
    pw = jax.nn.softmax(jnp.where(wmask, sw, NEG), axis=-1)
    o_win = jnp.einsum('bgrnqk,bgnkd->bgrnqd', pw.astype(vw.dtype), vw).reshape(bsz, G, R, t, dh)

    gates = jax.nn.sigmoid(gate_logits.astype(f32)).reshape(bsz, t, 3, G, R).transpose(2, 0, 3, 4, 1)[..., None]
    o = gates[0] * o_cmp + gates[1] * o_sel + gates[2] * o_win
    return o.transpose(0, 3, 1, 2, 4).reshape(bsz, t, NSA_WIDTH).astype(q.dtype)


def conv_ffn(x, w_up, conv_w, conv_b, w_down):
    hid = x @ w_up
    hid = lax.conv_general_dilated(hid, conv_w[:, None, :], window_strides=(1,),
                                   padding=[(CONV_WIDTH - 1, 0)],
                                   dimension_numbers=('NWC', 'WIO', 'NWC'),
                                   feature_group_count=hid.shape[-1]) + conv_b
    a, b = jnp.split(hid, 2, axis=-1)
    return (jax.nn.gelu(a) * b) @ w_down


def setup_inputs(seed: int = 0) -> dict:
    key = jax.random.key(seed)
    ks = jax.random.split(key, 40)
    f32 = jnp.float32
    L = DEPTH
    G, P, C = SSM_GROUPS, SSM_STATE, SSM_GROUP

    def nrm(k, shape, scale):
        return jax.random.normal(k, shape, f32) * scale

    return {
        'x': nrm(ks[0], (BATCH, SEQ, D_MODEL), 1.0),
        'p': nrm(ks[1], (DEPTH, BATCH, SEQ, PLE_DIM), 1.0),
        'g_mix': 1.0 + nrm(ks[2], (L, D_MODEL), 0.02),
        'w_in': nrm(ks[3], (L, D_MODEL, IN_WIDTH), D_MODEL ** -0.5),
        'ssm_a_re': -0.5 + nrm(ks[4], (L, G, P), 0.01),
        'ssm_a_im': math.pi * jnp.arange(P, dtype=f32) + nrm(ks[5], (L, G, P), 0.01),
        'ssm_log_dt': jax.random.uniform(ks[6], (L, G), f32, math.log(1e-3), math.log(1e-1)),
        'ssm_b_re': nrm(ks[7], (L, G, P, C), (2 * C) ** -0.5),
        'ssm_b_im': nrm(ks[8], (L, G, P, C), (2 * C) ** -0.5),
        'ssm_c_re': nrm(ks[9], (L, G, C, P), P ** -0.5),
        'ssm_c_im': nrm(ks[10], (L, G, C, P), P ** -0.5),
        'ssm_d': nrm(ks[11], (L, SSM_WIDTH), 1.0),
        'ssm_w_glu': nrm(ks[12], (L, SSM_WIDTH, SSM_WIDTH), SSM_WIDTH ** -0.5),
        'cmp_pe_k': nrm(ks[13], (L, CMP_BLOCK, HEAD_DIM), 0.02),
        'cmp_pe_v': nrm(ks[14], (L, CMP_BLOCK, HEAD_DIM), 0.02),
        'cmp_wk1': nrm(ks[15], (L, CMP_BLOCK * HEAD_DIM, CMP_HIDDEN), (CMP_BLOCK * HEAD_DIM) ** -0.5),
        'cmp_wk2': nrm(ks[16], (L, CMP_HIDDEN, HEAD_DIM), CMP_HIDDEN ** -0.5),
        'cmp_wv1': nrm(ks[17], (L, CMP_BLOCK * HEAD_DIM, CMP_HIDDEN), (CMP_BLOCK * HEAD_DIM) ** -0.5),
        'cmp_wv2': nrm(ks[18], (L, CMP_HIDDEN, HEAD_DIM), CMP_HIDDEN ** -0.5),
        'w_br_ssm': nrm(ks[19], (L, SSM_WIDTH, D_MODEL), SSM_WIDTH ** -0.5),
        'w_br_nsa': nrm(ks[20], (L, NSA_WIDTH, D_MODEL), NSA_WIDTH ** -0.5),
        'w_out': nrm(ks[21], (L, D_MODEL, D_MODEL), D_MODEL ** -0.5),
        'g_ffn': 1.0 + nrm(ks[22], (L, D_MODEL), 0.02),
        'w_up': nrm(ks[23], (L, D_MODEL, 2 * D_FF), D_MODEL ** -0.5),
        'conv_w': nrm(ks[24], (L, CONV_WIDTH, 2 * D_FF), CONV_WIDTH ** -0.5),
        'conv_b': nrm(ks[25], (L, 2 * D_FF), 0.01),
        'w_down': nrm(ks[26], (L, D_FF, D_MODEL), D_FF ** -0.5),
        'g_ple': 1.0 + nrm(ks[27], (L, D_MODEL), 0.02),
        'w_ple_gate': nrm(ks[28], (L, D_MODEL, D_MODEL), D_MODEL ** -0.5),
        'w_ple_proj': nrm(ks[29], (L, PLE_DIM, D_MODEL), PLE_DIM ** -0.5),
        'g_final': 1.0 + nrm(ks[30], (D_MODEL,), 0.02),
    }


def reference(x, p, g_mix, w_in, ssm_a_re, ssm_a_im, ssm_log_dt, ssm_b_re, ssm_b_im, ssm_c_re, ssm_c_im,
              ssm_d, ssm_w_glu, cmp_pe_k, cmp_pe_v, cmp_wk1, cmp_wk2, cmp_wv1, cmp_wv2, w_br_ssm, w_br_nsa,
              w_out, g_ffn, w_up, conv_w, conv_b, w_down, g_ple, w_ple_gate, w_ple_proj, g_final):
    h = x
    for i in range(DEPTH):
        n1 = rmsnorm(h, g_mix[i])
        z = n1 @ w_in[i]
        y_ssm = s5_mixer(z[..., OFF_SSM:OFF_Q], ssm_a_re[i], ssm_a_im[i], ssm_log_dt[i], ssm_b_re[i], ssm_b_im[i],
                         ssm_c_re[i], ssm_c_im[i], ssm_d[i], ssm_w_glu[i])
        y_nsa = nsa_mixer(z[..., OFF_Q:OFF_KV], z[..., OFF_KV:OFF_NSA_GATE], z[..., OFF_NSA_GATE:OFF_MERGE],
                          cmp_pe_k[i], cmp_pe_v[i], cmp_wk1[i], cmp_wk2[i], cmp_wv1[i], cmp_wv2[i])
        merge = jax.nn.sigmoid(z[..., OFF_MERGE:].astype(jnp.float32))
        mixed = merge[..., :D_MODEL] * (y_ssm @ w_br_ssm[i]) + merge[..., D_MODEL:] * (y_nsa @ w_br_nsa[i])
        h = h + mixed.astype(h.dtype) @ w_out[i]
        h = h + conv_ffn(rmsnorm(h, g_ffn[i]), w_up[i], conv_w[i], conv_b[i], w_down[i])
        gate = jax.nn.sigmoid(rmsnorm(h, g_ple[i]) @ w_ple_gate[i])
        h = h + gate * (p[i] @ w_ple_proj[i])
    return rmsnorm(h, g_final)
```

```python
import math
import numpy as np
import concourse.bass as bass
import concourse.mybir as mybir
from concourse.bass_utils import run_bass_kernel_spmd
from contextlib import ExitStack

F32 = mybir.dt.float32
BF16 = mybir.dt.bfloat16
AF = mybir.ActivationFunctionType
ALU = mybir.AluOpType
NDQ = 8

T = 4096
D = 1024
NCH = 8
CT = 512
IN_W = 3864
OFF_Q, OFF_KV, OFF_G, OFF_M = 512, 1024, 1792, 1816
DFF = 2816
EPS = 1e-6
LS = 256
NEGM = -30000.0


class Prog:
    def __init__(self, nc, es):
        self.nc = nc
        self.eng = {"pe": nc.tensor, "act": nc.scalar, "dve": nc.vector, "pool": nc.gpsimd, "sp": nc.sync}
        self.sems = {}
        for e in ("pe", "act", "dve", "pool"):
            self.sems[e] = es.enter_context(nc.semaphore("s_" + e))
        for q in ("sp", "pool", "act"):
            for i in range(NDQ):
                self.sems[(q, "d", i)] = es.enter_context(nc.semaphore("d_%s%d" % (q, i)))
        self.cnt = {e: 0 for e in self.eng}
        self.dcnt = {"sp": 0, "pool": 0, "act": 0}
        self.seen = {e: {} for e in self.eng}
        self.reg = {}
        self.last = {}

    def _wait(self, stream, toks, dma):
        e = self.eng[stream]
        for sk, v in toks:
            if sk == "pe" and stream == "pe" and not dma:
                continue
            if self.seen[stream].get(sk, 0) >= v:
                continue
            e.wait_ge(self.sems[sk], v)
            self.seen[stream][sk] = v

    def op(self, stream, fn, reads=(), writes=(), dma=False):
        toks = {}

        def add(t):
            if t is not None and toks.get(t[0], 0) < t[1]:
                toks[t[0]] = t[1]

        for k in reads:
            r = self.reg.get(k)
            if r is not None:
                add(r[0])
        for k in writes:
            r = self.reg.get(k)
            if r is not None:
                add(r[0])
                for sk, v in r[1].items():
                    add((sk, v))
        if dma:
            i = self.dcnt[stream]
            semkey = (stream, "d", i % NDQ)
            val = 16 * (i // NDQ + 1)
            if i >= NDQ:
                add((semkey, 16 * (i // NDQ)))
            self.dcnt[stream] = i + 1
        else:
            semkey = stream
            val = self.cnt[stream] + 1
            self.cnt[stream] = val
        self._wait(stream, sorted(toks.items(), key=lambda x: str(x[0])), dma)
        ins = fn(self.eng[stream])
        ins.then_inc(self.sems[semkey], 16 if dma else 1)
        tok = (semkey, val)
        for k in writes:
            self.reg[k] = [tok, {}]
        for k in reads:
            r = self.reg.setdefault(k, [None, {}])
            if r[1].get(semkey, 0) < val:
                r[1][semkey] = val
        self.last[semkey] = val
        return tok

    def barrier(self):
        for s in self.eng:
            self._wait(s, sorted(self.last.items(), key=lambda x: str(x[0])), True)
        self.reg = {}

    def finish(self):
        self._wait("sp", sorted(self.last.items(), key=lambda x: str(x[0])), True)


def build(stop_after="C", debug=False):
    nc = bass.Bass("TRN2", target_bir_lowering=False)

    def din(name, shape, dt=F32):
        return nc.dram_tensor(name, list(shape), dt, kind="ExternalInput").ap()

    def dscr(name, shape, dt):
        return nc.dram_tensor(name, list(shape), dt).ap()

    x = din("x", [T, D])
    pp = din("p", [T, 256])
    w_in = din("w_in", [D, IN_W])
    gmix = din("gmix", [128, D])
    gffn = din("gffn", [128, D])
    gple = din("gple", [128, D])
    gfin = din("gfin", [128, D])
    are = din("are", [128, 16])
    aim = din("aim", [128, 16])
    ldt = din("ldt", [128, 16])
    bnre = din("bnre", [128, 16, 128])
    bnim = din("bnim", [128, 16, 128])
    cnre = din("cnre", [128, 16, 128])
    cnim = din("cnim", [128, 16, 128])
    dsk = din("dsk", [128, 4])
    w_glu = din("w_glu", [512, 512])
    w_bs = din("w_bs", [512, D])
    w_bn = din("w_bn", [512, D])
    w_out = din("w_out", [D, D])
    w_up = din("w_up", [D, 2 * DFF])
    cvw = din("cvw", [128, 44, 3])
    cvb = din("cvb", [128, 44])
    w_down = din("w_down", [DFF, D])
    w_pg = din("w_pg", [D, D])
    w_pp = din("w_pp", [256, D])
    wk1 = din("wk1", [128, 32, 128])
    wv1 = din("wv1", [128, 32, 128])
    wk2 = din("wk2", [128, 64])
    wv2 = din("wv2", [128, 64])
    pekT = din("pekT", [128, 32])
    pevT = din("pevT", [128, 32])
    ident = din("ident", [128, 128])
    ekey = din("ekey", [64, T])
    cmaskc = din("cmaskc", [128, 17, 4, 128])
    vmA = din("vmA", [128, 128])
    vmB = din("vmB", [128, 128])
    mdiag = din("mdiag", [128, 2, 4, 128])
    ovx = din("ovx", [128, 2, 2, 64])

    out = nc.dram_tensor("out", [T, D], F32, kind="ExternalOutput").ap()
    dbg = {}

    def dout(name, shape, dt=F32):
        dbg[name] = nc.dram_tensor(name, list(shape), dt, kind="ExternalOutput").ap()
        return dbg[name]

    QT_d = dscr("QT_d", [8, 64, T], BF16)
    KCMP_d = dscr("KCMP_d", [128, T], BF16)
    VCMP_d = dscr("VCMP_d", [128, T], BF16)
    KSEL_d = dscr("KSEL_d", [2, 64, T], BF16)
    KWIN_d = dscr("KWIN_d", [2, 64, T], BF16)
    VSEL_d = dscr("VSEL_d", [T, 128], BF16)
    VWIN_d = dscr("VWIN_d", [T, 128], BF16)
    M2_d = dscr("M2_d", [D, T], BF16)
    M1_d = dscr("M1_d", [D, T], BF16)
    U_d = dscr("U_d", [512, T], F32)
    MIXA_d = dscr("MIXA_d", [D, T], BF16)
    H_d = dscr("H_d", [T, D], F32)
    ACT_d = dscr("ACT_d", [DFF, T], BF16)

    with ExitStack() as es0:
        P = Prog(nc, es0)

        def dma(q, out_, in_, r, w):
            return P.op(q, lambda e: e.dma_start(out=out_, in_=in_), r, w, dma=True)

        def act(out_, in_, func, r, w, **kw):
            return P.op("act", lambda e: e.activation(out=out_, in_=in_, func=func, **kw), r, w)

        def mm(out_, lhsT, rhs, start, stop, r, w):
            return P.op("pe", lambda e: e.matmul(out_, lhsT, rhs, start=start, stop=stop), r, w)

        def tr(out_, in_, idn, r, w):
            return P.op("pe", lambda e: e.transpose(out=out_, in_=in_, identity=idn), r, w)

        def tt(eng, out_, in0, in1, op, r, w):
            return P.op(eng, lambda e: e.tensor_tensor(out=out_, in0=in0, in1=in1, op=op), r, w)

        def ts(eng, out_, in0, s1, op0, r, w, s2=None, op1=None):
            if op1 is None:
                return P.op(eng, lambda e: e.tensor_scalar(out=out_, in0=in0, scalar1=s1, scalar2=None, op0=op0), r, w)
            return P.op(eng, lambda e: e.tensor_scalar(out=out_, in0=in0, scalar1=s1, scalar2=s2, op0=op0, op1=op1), r, w)

        def stt(out_, in0, sc, in1, op0, op1, r, w):
            return P.op("dve", lambda e: e.scalar_tensor_tensor(out=out_, in0=in0, scalar=sc, in1=in1, op0=op0, op1=op1), r, w)

        def cp(eng, out_, in_, r, w):
            return P.op(eng, lambda e: e.tensor_copy(out_, in_), r, w)

        def ms(eng, ap, val, w):
            return P.op(eng, lambda e: e.memset(ap, val), (), w)

        nmc = [0]

        def SB(es, name, shape, dt=F32):
            nmc[0] += 1
            return es.enter_context(nc.sbuf_tensor("%s_%d" % (name, nmc[0]), list(shape), dt))

        ps = [es0.enter_context(nc.psum_tensor("ps%d" % i, [128, 512], F32)) for i in range(8)]
        PSK = ["ps%d" % i for i in range(8)]

        idf = SB(es0, "idf", [128, 128])
        idb = SB(es0, "idb", [128, 128], BF16)
        gates = SB(es0, "gates", [128, 32, 24])
        dma("sp", idf[:], ident[:, :], [], ["idf"])
        dma("pool", idb[:], ident[:, :], [], ["idb"])

        with ExitStack() as es:
            Win = SB(es, "Win", [128, 8, IN_W], BF16)
            gm = SB(es, "gm", [128, D])
            dma("sp", gm[:], gmix[:, :], [], ["gm"])
            WB = [0, 1024, 1816, 2840, 3864]
            for bi in range(4):
                dma("pool", Win[:, :, WB[bi]:WB[bi + 1]], w_in.rearrange("(k p) n -> p k n", p=128)[:, :, WB[bi]:WB[bi + 1]], [], ["Win%d" % bi])

            def wkey(c0):
                return "Win%d" % max(i for i in range(4) if WB[i] <= c0)
            xs = SB(es, "xs", [128, 4, D])
            xb = SB(es, "xb", [128, 2, 4, D], BF16)
            ss = SB(es, "ss", [128, 8])
            n1T = SB(es, "n1T", [128, 2, 8, CT], BF16)
            ust = SB(es, "ust", [128, 4, CT])
            qst = SB(es, "qst", [128, 4, CT], BF16)
            kvst = SB(es, "kvst", [128, 2, CT], BF16)
            k2st = SB(es, "k2st", [128, 2, CT], BF16)
            vtmp = SB(es, "vtmp", [128, 2, CT], BF16)
            vst = SB(es, "vst", [128, 2, 4, 128], BF16)
            gT = SB(es, "gT", [24, CT])
            mst = SB(es, "mst", [128, 2, 8, CT], BF16)

            tiles = []
            for i in range(4):
                tiles.append((i * 128, 128, "u", i))
            for h in range(4):
                tiles.append((OFF_Q + h * 128, 128, "q", h))
            tiles.append((OFF_KV + 0, 128, "kcmp", 0))
            tiles.append((OFF_KV + 128, 128, "vcmp", 0))
            tiles.append((OFF_KV + 256, 128, "ksel", 0))
            tiles.append((OFF_KV + 384, 128, "vsel", 0))
            tiles.append((OFF_KV + 512, 128, "kwin", 0))
            tiles.append((OFF_KV + 640, 128, "vwin", 0))
            tiles.append((OFF_G, 24, "gate", 0))
            for j in range(16):
                tiles.append((OFF_M + j * 128, 128, "m", j))

            def prologue(c):
                t0_ = c * CT
                pb_ = c % 2
                dma("sp", xs[:], x[t0_:t0_ + CT, :].rearrange("(s p) d -> p s d", p=128), [], ["xs"])
                for s in range(4):
                    act(xb[:, pb_, s, :], xs[:, s, :], AF.Square, ["xs"], ["xb%d_%d" % (pb_, s), "ss"], accum_out=ss[:, s:s + 1])
                act(ss[:, 4:8], ss[:, 0:4], AF.Sqrt, ["ss"], ["ss2"], scale=1.0 / D, bias=EPS)
                P.op("dve", lambda e: e.reciprocal(ss[:, 4:8], ss[:, 4:8]), ["ss2"], ["ss2"])
                for s in range(4):
                    stt(xb[:, pb_, s, :], xs[:, s, :], ss[:, 4 + s:5 + s], gm[:], ALU.mult, ALU.mult, ["xs", "ss2", "gm"], ["xb%d_%d" % (pb_, s)])
                for kt in range(8):
                    b = 2 + (kt % 2)
                    pv = ps[b][:, 0:256].bitcast(BF16)
                    for s in range(4):
                        tr(pv[:, s * 128:(s + 1) * 128], xb[:, pb_, s, kt * 128:(kt + 1) * 128], idb[:], ["xb%d_%d" % (pb_, s), "idb"], [PSK[b]])
                    if kt % 2 == 0:
                        act(n1T[:, pb_, kt, :], pv, AF.Copy, [PSK[b]], ["n1T%d_%d" % (pb_, kt)])
                    else:
                        cp("dve", n1T[:, pb_, kt, :], pv, [PSK[b]], ["n1T%d_%d" % (pb_, kt)])

            prologue(0)
            for c in range(NCH):
                t0 = c * CT
                pbc = c % 2
                for ti, (c0, wd, kind, idx) in enumerate(tiles):
                    b = ti % 2
                    if ti == 12 and c + 1 < NCH:
                        prologue(c + 1)
                    for kt in range(8):
                        mm(ps[b][0:wd, :], Win[:, kt, c0:c0 + wd], n1T[:, pbc, kt, :], kt == 0, kt == 7,
                           [wkey(c0), "n1T%d_%d" % (pbc, kt)], [PSK[b]])
                    src = ps[b][0:wd, :]
                    if kind == "u":
                        cp("dve", ust[:, idx, :], src, [PSK[b]], ["ust"])
                        if idx == 3:
                            dma("pool", U_d[:, t0:t0 + CT].rearrange("(o p) t -> p o t", p=128), ust[:], ["ust"], [])
                    elif kind == "q":
                        act(qst[:, idx, :], src, AF.Copy, [PSK[b]], ["qst"], scale=0.125)
                        if idx == 3:
                            for hh in range(2):
                                dma("pool", QT_d[hh::2, :, t0:t0 + CT].rearrange("h d t -> d h t"), qst[hh * 64:(hh + 1) * 64], ["qst"], [])
                    elif kind in ("kcmp", "vcmp"):
                        j = 0 if kind == "kcmp" else 1
                        act(kvst[:, j, :], src, AF.Copy, [PSK[b]], ["kvst%d" % j])
                        dma("pool", (KCMP_d if j == 0 else VCMP_d)[:, t0:t0 + CT], kvst[:, j, :], ["kvst%d" % j], [])
                    elif kind in ("ksel", "kwin"):
                        j = 0 if kind == "ksel" else 1
                        cp("dve", k2st[:, j, :], src, [PSK[b]], ["k2st%d" % j])
                        for gg in range(2):
                            dma("pool", (KSEL_d if kind == "ksel" else KWIN_d)[gg, :, t0:t0 + CT], k2st[gg * 64:(gg + 1) * 64, j, :], ["k2st%d" % j], [])
                    elif kind in ("vsel", "vwin"):
                        j = 0 if kind == "vsel" else 1
                        act(vtmp[:, j, :], src, AF.Copy, [PSK[b]], ["vtmp%d" % j])
                        pv = ps[4][:, 0:256].bitcast(BF16)
                        for s in range(4):
                            tr(pv[:, s * 128:(s + 1) * 128], vtmp[:, j, s * 128:(s + 1) * 128], idb[:], ["vtmp%d" % j, "idb"], [PSK[4]])
                        cp("dve", vst[:, j].rearrange("p s c -> p (s c)"), pv, [PSK[4]], ["vst%d" % j])
                        dma("pool", (VSEL_d if j == 0 else VWIN_d)[t0:t0 + CT, :].rearrange("(s p) c -> p s c", p=128), vst[:, j], ["vst%d" % j], [])
                    elif kind == "gate":
                        act(gT[:, :], src, AF.Sigmoid, [PSK[b]], ["gT"])
                        for s in range(4):
                            tr(ps[5][:, s * 24:(s + 1) * 24], gT[:, s * 128:(s + 1) * 128], idf[0:24, 0:24], ["gT", "idf"], [PSK[5]])
                        cp("dve", gates[:, c * 4:(c + 1) * 4, :].rearrange("p s g -> p (s g)"), ps[5][:, 0:96], [PSK[5]], ["gates"])
                    else:
                        h2 = idx // 8
                        act(mst[:, h2, idx % 8, :], src, AF.Sigmoid, [PSK[b]], ["mst%d" % h2])
                        if idx % 8 == 7:
                            dma("pool", (M1_d if h2 == 0 else M2_d)[:, t0:t0 + CT].rearrange("(o p) t -> p o t", p=128), mst[:, h2], ["mst%d" % h2], [])
        P.barrier()
        if stop_after == "A1":
            P.finish()
            return nc, dbg

        esKV = ExitStack()
        KE = SB(esKV, "KE", [128, 2, T], BF16)
        Vs = SB(esKV, "Vs", [128, 32, 2, 66], BF16)
        for g in range(2):
            dma("pool", KE[64:128, g, :], ekey[:, :], [], ["KE"])
            dma("sp", KE[0:64, g, :], KSEL_d[g, :, :], [], ["KE"])
        ms("dve", Vs[:, :, :, 64:66], 1.0, ["Vs"])
        for g in range(2):
            dma("sp", Vs[:, :, g, 0:64], VSEL_d.rearrange("(k p) c -> p k c", p=128)[:, :, g * 64:(g + 1) * 64], [], ["Vs"])

        with ExitStack() as es:
            Ec = SB(es, "Ec", [128, 16, LS])
            Es = SB(es, "Es", [128, 16, LS])
            rr = SB(es, "rr", [128, 16])
            BT = SB(es, "BT", [128, 2, 16, 128], BF16)
            CTw = SB(es, "CTw", [128, 3, 16, 128], BF16)
            dskip = SB(es, "dskip", [128, 4])
            Wglu = SB(es, "Wglu", [128, 4, 512], BF16)
            Wbs = SB(es, "Wbs", [128, 4, D], BF16)
            dma("sp", dskip[:], dsk[:, :], [], ["dskip"])
            dma("pool", Wglu[:], w_glu.rearrange("(k p) n -> p k n", p=128), [], ["Wglu"])
            dma("pool", Wbs[:], w_bs.rearrange("(k p) n -> p k n", p=128), [], ["Wbs"])
            with ExitStack() as esp:
                v16 = SB(esp, "v16", [128, 24, 16])
                V = lambda i: v16[:, i, :]
                Bn = SB(esp, "Bn", [128, 2, 16, 128])
                Bw = SB(esp, "Bw", [128, 3, 16, 128])
                tmpE = SB(esp, "tmpE", [128, 2, 16, LS // 2])
                hp = SB(esp, "hp", [128, 1])
                dma("sp", V(0), are[:, :], [], ["v0"])
                dma("sp", V(1), aim[:, :], [], ["v1"])
                dma("sp", V(2), ldt[:, :], [], ["v2"])
                dma("sp", Bn[:, 0], bnre[:, :, :], [], ["Bn0"])
                dma("sp", Bn[:, 1], bnim[:, :, :], [], ["Bn1"])
                ts("dve", V(3), V(0), -1e-4, ALU.min, ["v0"], ["v3"])
                act(V(4), V(2), AF.Exp, ["v2"], ["v4"])
                tt("dve", V(5), V(3), V(4), ALU.mult, ["v3", "v4"], ["v5"])
                act(rr[:], V(5), AF.Exp, ["v5"], ["rr"])
                tt("dve", V(6), V(1), V(4), ALU.mult, ["v1", "v4"], ["v6"])
                ms("dve", hp[:], math.pi / 2, ["hp"])
                act(V(7), V(6), AF.Sin, ["v6", "hp"], ["v7"], scale=1.0 / 32, bias=hp[:, 0:1])
                act(V(8), V(6), AF.Sin, ["v6"], ["v8"], scale=1.0 / 32)
                for _ in range(5):
                    tt("dve", V(9), V(7), V(7), ALU.mult, ["v7"], ["v9"])
                    tt("dve", V(10), V(8), V(8), ALU.mult, ["v8"], ["v10"])
                    tt("dve", V(11), V(7), V(8), ALU.mult, ["v7", "v8"], ["v11"])
                    tt("dve", V(7), V(9), V(10), ALU.subtract, ["v9", "v10"], ["v7"])
                    ts("dve", V(8), V(11), 2.0, ALU.mult, ["v11"], ["v8"])
                tt("dve", V(12), rr[:], V(7), ALU.mult, ["rr", "v7"], ["v12"])
                ts("dve", V(12), V(12), -1.0, ALU.add, ["v12"], ["v12"])
                tt("dve", V(13), rr[:], V(8), ALU.mult, ["rr", "v8"], ["v13"])
                tt("dve", V(14), V(3), V(3), ALU.mult, ["v3"], ["v14"])
                tt("dve", V(15), V(1), V(1), ALU.mult, ["v1"], ["v15"])
                tt("dve", V(14), V(14), V(15), ALU.add, ["v14", "v15"], ["v14"])
                P.op("dve", lambda e: e.reciprocal(V(14), V(14)), ["v14"], ["v14"])
                tt("dve", V(16), V(12), V(3), ALU.mult, ["v12", "v3"], ["v16"])
                tt("dve", V(18), V(13), V(1), ALU.mult, ["v13", "v1"], ["v18"])
                tt("dve", V(16), V(16), V(18), ALU.add, ["v16", "v18"], ["v16"])
                tt("dve", V(16), V(16), V(14), ALU.mult, ["v16", "v14"], ["v16"])
                tt("dve", V(17), V(13), V(3), ALU.mult, ["v13", "v3"], ["v17"])
                tt("dve", V(18), V(12), V(1), ALU.mult, ["v12", "v1", "v16"], ["v18"])
                tt("dve", V(17), V(17), V(18), ALU.subtract, ["v17", "v18"], ["v17"])
                tt("dve", V(17), V(17), V(14), ALU.mult, ["v17", "v14"], ["v17"])
                cp("dve", Ec[:, :, 0:1], V(7).unsqueeze(2), ["v7"], ["Ec"])
                cp("dve", Es[:, :, 0:1], V(8).unsqueeze(2), ["v8"], ["Es"])
                cp("dve", V(19), V(7), ["v7"], ["v19"])
                cp("dve", V(20), V(8), ["v8"], ["v20"])
                n = 1
                while n < LS:
                    cb = V(19).unsqueeze(2).to_broadcast([128, 16, n])
                    sb_ = V(20).unsqueeze(2).to_broadcast([128, 16, n])
                    tt("dve", tmpE[:, 0, :, 0:n], Ec[:, :, 0:n], cb, ALU.mult, ["Ec", "v19"], ["tE0"])
                    tt("dve", tmpE[:, 1, :, 0:n], Es[:, :, 0:n], sb_, ALU.mult, ["Es", "v20"], ["tE1"])
                    tt("dve", Ec[:, :, n:2 * n], tmpE[:, 0, :, 0:n], tmpE[:, 1, :, 0:n], ALU.subtract, ["tE0", "tE1", "Ec"], ["Ec"])
                    tt("dve", tmpE[:, 0, :, 0:n], Ec[:, :, 0:n], sb_, ALU.mult, ["Ec", "v20"], ["tE0"])
                    tt("dve", tmpE[:, 1, :, 0:n], Es[:, :, 0:n], cb, ALU.mult, ["Es", "v19"], ["tE1"])
                    tt("dve", Es[:, :, n:2 * n], tmpE[:, 0, :, 0:n], tmpE[:, 1, :, 0:n], ALU.add, ["tE0", "tE1", "Es"], ["Es"])
                    tt("dve", V(9), V(19), V(19), ALU.mult, ["v19"], ["v9"])
                    tt("dve", V(10), V(20), V(20), ALU.mult, ["v20"], ["v10"])
                    tt("dve", V(11), V(19), V(20), ALU.mult, ["v19", "v20"], ["v11"])
                    tt("dve", V(19), V(9), V(10), ALU.subtract, ["v9", "v10"], ["v19"])
                    ts("dve", V(20), V(11), 2.0, ALU.mult, ["v11"], ["v20"])
                    n *= 2
                frb = V(16).unsqueeze(2).to_broadcast([128, 16, 128])
                fib = V(17).unsqueeze(2).to_broadcast([128, 16, 128])
                tt("dve", Bw[:, 0], Bn[:, 1], fib, ALU.mult, ["Bn1", "v17"], ["Bw0"])
                tt("dve", Bw[:, 1], Bn[:, 0], frb, ALU.mult, ["Bn0", "v16"], ["Bw1"])
                tt("dve", Bw[:, 1], Bw[:, 1], Bw[:, 0], ALU.subtract, ["Bw0", "Bw1"], ["Bw1"])
                tt("dve", Bw[:, 0], Bn[:, 1], frb, ALU.mult, ["Bn1", "v16", "Bw1"], ["Bw0"])
                tt("dve", Bw[:, 2], Bn[:, 0], fib, ALU.mult, ["Bn0", "v17"], ["Bw2"])
                tt("dve", Bw[:, 2], Bw[:, 2], Bw[:, 0], ALU.add, ["Bw0", "Bw2"], ["Bw2"])
                for ri in range(2):
                    for ct in range(16):
                        b = 2 + (ct % 2)
                        tr(ps[b][:, 0:128], Bw[:, 1 + ri, ct, :], idf[:], ["Bw%d" % (1 + ri), "idf"], [PSK[b]])
                        act(BT[:, ri, ct, :], ps[b][:, 0:128], AF.Copy, [PSK[b]], ["BT"])
                dma("sp", Bn[:, 0], cnre[:, :, :], ["Bn0"], ["Bn0"])
                dma("sp", Bn[:, 1], cnim[:, :, :], ["Bn1"], ["Bn1"])
                for ri in range(2):
                    for ct in range(16):
                        b = 2 + (ct % 2)
                        tr(ps[b][:, 0:128], Bn[:, ri, ct, :], idf[:], ["Bn%d" % ri, "idf"], [PSK[b]])
                        act(CTw[:, ri, ct, :], ps[b][:, 0:128], AF.Copy, [PSK[b]], ["CT"], scale=(1.0 if ri == 0 else -1.0))
                        if ri == 0:
                            act(CTw[:, 2, ct, :], ps[b][:, 0:128], AF.Copy, [PSK[b]], ["CT"], scale=-1.0)
            P.barrier()
            EEb = SB(es, "EEb", [128, 16, 3, LS], BF16)
            EsN = SB(es, "EsN", [128, 16, 2, LS], BF16)
            act(EEb[:, :, 0, :], Ec[:], AF.Copy, ["Ec"], ["EEb"])
            act(EEb[:, :, 1, :], Es[:], AF.Copy, ["Es"], ["EEb"])
            act(EEb[:, :, 2, :], Ec[:], AF.Copy, ["Ec"], ["EEb"])
            act(EsN[:, :, 0, :], Es[:], AF.Copy, ["Es"], ["EsN"])
            act(EsN[:, :, 1, :], Es[:], AF.Copy, ["Es"], ["EsN"], scale=-1.0)
            bub = SB(es, "bub", [128, 2, 4, LS], BF16)
            uTf = SB(es, "uTf", [128, 4, CT])
            uTb = SB(es, "uTb", [128, 4, CT], BF16)
            m1T = SB(es, "m1T", [128, 8, CT], BF16)
            sw = SB(es, "sw", [128, 2, 1, 8])
            swb = SB(es, "swb", [128, 2, 8, LS], BF16)
            sbf = SB(es, "sbf", [128, 2, 4, LS], BF16)
            stc = SB(es, "stc", [128, 2, 16])
            wl = SB(es, "wl", [128, 2, 16])
            wt = SB(es, "wt", [128, 4, 16])
            ysb = SB(es, "ysb", [128, LS])
            yg = SB(es, "yg", [128, 4, CT])
            ygb = SB(es, "ygb", [128, 4, CT], BF16)
            sg = SB(es, "sg", [128, CT])
            yssm = SB(es, "yssm", [128, 4, CT], BF16)
            mixst = SB(es, "mixst", [128, 8, CT], BF16)
            ms("dve", stc[:], 0.0, ["stc"])
            if debug:
                d_y = dout("d_yssm", [512, T], BF16)
            for c in range(NCH):
                t0 = c * CT
                dma("sp", uTf[:], U_d[:, t0:t0 + CT].rearrange("(o p) t -> p o t", p=128), [], ["uTf%d" % i for i in range(4)])
                dma("sp", m1T[:], M1_d[:, t0:t0 + CT].rearrange("(o p) t -> p o t", p=128), [], ["m1T"])
                for i in range(4):
                    act(uTb[:, i, :], uTf[:, i, :], AF.Copy, ["uTf%d" % i], ["uTb%d" % i])
                for sc in range(CT // LS):
                    cs = slice(sc * LS, (sc + 1) * LS)

                    def bu(ct):
                        b_ = 4 + (ct % 2)
                        mm(ps[b_][:, 0:LS], BT[:, 0, ct, :], uTb[:, ct // 4, cs], True, True, ["BT", "uTb%d" % (ct // 4)], [PSK[b_]])
                        mm(ps[b_][:, LS:2 * LS], BT[:, 1, ct, :], uTb[:, ct // 4, cs], True, True, ["BT", "uTb%d" % (ct // 4)], [PSK[b_]])
                    def evac(ct):
                        b_ = 4 + (ct % 2)
                        p_ = ct % 2
                        act(bub[:, p_, 0:2, :].rearrange("p a t -> p (a t)"), ps[b_][:, 0:2 * LS], AF.Copy, [PSK[b_]], ["bub%d" % p_])
                        act(bub[:, p_, 2, :], ps[b_][:, LS:2 * LS], AF.Copy, [PSK[b_]], ["bub%d" % p_])
                        act(bub[:, p_, 3, :], ps[b_][:, 0:LS], AF.Copy, [PSK[b_]], ["bub%d" % p_])
                    bu(0)
                    bu(1)
                    evac(0)
                    bu(2)
                    for ct in range(16):
                        b = 4 + (ct % 2)
                        yt = ct // 4
                        par = ct % 2
                        bre = ps[b][:, 0:LS]
                        bim = ps[b][:, LS:2 * LS]
                        W = lambda i: sw[:, par, i, :]
                        K = lambda i: "sw%d_%d" % (par, i)
                        Wb = lambda i: swb[:, par, i, :]
                        Kb = lambda i: "swb%d_%d" % (par, i)
                        bk_ = "bub%d" % par
                        tt("dve", swb[:, par, 0:2, :], bub[:, par, 0:2, :], EEb[:, ct, 0:3:2, :], ALU.mult, [bk_, "EEb"], ["X%d" % par])
                        tt("dve", swb[:, par, 2:4, :], bub[:, par, 2:4, :], EsN[:, ct, :, :], ALU.mult, [bk_, "EsN"], ["Y%d" % par])
                        tt("dve", swb[:, par, 4:6, :], swb[:, par, 0:2, :], swb[:, par, 2:4, :], ALU.add, ["X%d" % par, "Y%d" % par], ["M%d" % par])
                        if ct + 1 < 16:
                            evac(ct + 1)
                            if ct + 3 < 16:
                                bu(ct + 3)
                        rb = rr[:, ct:ct + 1].to_broadcast([128, LS])
                        P.op("dve", lambda e: e.tensor_tensor_scan(out=Wb(7), data0=rb, data1=Wb(5), initial=stc[:, 1, ct:ct + 1], op0=ALU.mult, op1=ALU.add),
                             ["M%d" % par, "rr", "stc"], [Kb(7)])
                        P.op("dve", lambda e: e.tensor_tensor_scan(out=Wb(6), data0=rb, data1=Wb(4), initial=stc[:, 0, ct:ct + 1], op0=ALU.mult, op1=ALU.add),
                             ["M%d" % par, "rr", "stc"], [Kb(6)])
                        act(wl[:, 0, ct:ct + 1], swb[:, par, 6, LS - 1:LS], AF.Copy, [Kb(6)], ["wl%d" % ct])
                        act(wl[:, 1, ct:ct + 1], swb[:, par, 7, LS - 1:LS], AF.Copy, [Kb(7)], ["wl%d" % ct])
                        Q = lambda i: sbf[:, par, i, :]
                        QK = lambda i: "sbf%d_%d" % (par, i // 2)
                        tt("dve", sbf[:, par, 0:2, :], swb[:, par, 6:8, :], EEb[:, ct, 0:2, :], ALU.mult, [Kb(6), Kb(7), "EEb"], [QK(0)])
                        tt("dve", sbf[:, par, 2:4, :], swb[:, par, 6:8, :], EEb[:, ct, 1:3, :], ALU.mult, [Kb(6), Kb(7), "EEb"], [QK(2)])
                        yb = 6 + (yt % 2)
                        mm(ps[yb][:, 0:LS], CTw[:, 0, ct, :], Q(0), ct % 4 == 0, False, ["CT", QK(0)], [PSK[yb]])
                        mm(ps[yb][:, 0:LS], CTw[:, 2, ct, :], Q(1), False, False, ["CT", QK(1)], [PSK[yb]])
                        mm(ps[yb][:, 0:LS], CTw[:, 1, ct, :], Q(2), False, False, ["CT", QK(2)], [PSK[yb]])
                        mm(ps[yb][:, 0:LS], CTw[:, 1, ct, :], Q(3), False, ct % 4 == 3, ["CT", QK(3)], [PSK[yb]])
                        if ct % 4 == 3:
                            stt(ysb[:], uTf[:, yt, cs], dskip[:, yt:yt + 1], ps[yb][:, 0:LS], ALU.mult, ALU.add,
                                ["uTf%d" % yt, "dskip", PSK[yb]], ["ysb"])
                            act(yg[:, yt, cs], ysb[:], AF.Gelu_apprx_tanh, ["ysb"], ["yg%d" % yt])
                            act(ygb[:, yt, cs], yg[:, yt, cs], AF.Copy, ["yg%d" % yt], ["ygb%d" % yt])
                    wks = ["wl%d" % i for i in range(16)]
                    EcL = Ec[:, :, LS - 1]
                    EsL = Es[:, :, LS - 1]
                    tt("dve", wt[:, 0, :], wl[:, 0, :], EcL, ALU.mult, wks + ["Ec"], ["wt0"])
                    tt("dve", wt[:, 1, :], wl[:, 1, :], EsL, ALU.mult, wks + ["Es"], ["wt1"])
                    tt("dve", wt[:, 2, :], wl[:, 1, :], EcL, ALU.mult, wks + ["Ec"], ["wt2"])
                    tt("dve", wt[:, 3, :], wl[:, 0, :], EsL, ALU.mult, wks + ["Es"], ["wt3"])
                    tt("dve", stc[:, 0, :], wt[:, 0, :], wt[:, 1, :], ALU.subtract, ["wt0", "wt1"], ["stc"])
                    tt("dve", stc[:, 1, :], wt[:, 2, :], wt[:, 3, :], ALU.add, ["wt2", "wt3"], ["stc"])
                for ot in range(4):
                    b = ot % 2
                    for kt in range(4):
                        mm(ps[b][:, :], Wglu[:, kt, ot * 128:(ot + 1) * 128], ygb[:, kt, :], kt == 0, kt == 3, ["Wglu", "ygb%d" % kt], [PSK[b]])
                    act(sg[:], ps[b][:, :], AF.Sigmoid, [PSK[b]], ["sg"])
                    tt("dve", yssm[:, ot, :], yg[:, ot, :], sg[:], ALU.mult, ["yg%d" % ot, "sg"], ["yssm%d" % ot])
                if debug:
                    dma("sp", d_y[:, t0:t0 + CT].rearrange("(o p) t -> p o t", p=128), yssm[:], ["yssm%d" % i for i in range(4)], [])
                for ot in range(8):
                    b = ot % 2
                    for kt in range(4):
                        mm(ps[b][:, :], Wbs[:, kt, ot * 128:(ot + 1) * 128], yssm[:, kt, :], kt == 0, kt == 3, ["Wbs", "yssm%d" % kt], [PSK[b]])
                    tt("dve", mixst[:, ot, :], ps[b][:, :], m1T[:, ot, :], ALU.mult, [PSK[b], "m1T"], ["mixst"])
                dma("pool", MIXA_d[:, t0:t0 + CT].rearrange("(o p) t -> p o t", p=128), mixst[:], ["mixst"], [])
        P.barrier()

        if stop_after == "A2":
            if debug:
                dm = dout("d_MIXA", [D, T], BF16)
                dma("sp", dm[:, :], MIXA_d[:, :], [], [])
            P.finish()
            return nc, dbg

        with ExitStack() as es:
            kwT = SB(es, "kwT", [128, 2, T], BF16)
            Vw = SB(es, "Vw", [128, 32, 2, 66], BF16)
            ms("dve", kwT[64:128], 0.0, ["kwT"])
            ms("dve", Vw[:, :, :, 64:66], 1.0, ["Vw"])
            for g in range(2):
                dma("sp", kwT[0:64, g, :], KWIN_d[g, :, :], [], ["kwT"])
                dma("sp", Vw[:, :, g, 0:64], VWIN_d.rearrange("(k p) c -> p k c", p=128)[:, :, g * 64:(g + 1) * 64], [], ["Vw"])
            kcT = SB(es, "kcT", [128, 2, 256], BF16)
            VCX = SB(es, "VCX", [128, 2, 2, 130], BF16)
            cmk = SB(es, "cmk", [128, 17, 4, 128], BF16)
            vA = SB(es, "vA", [128, 128])
            vB = SB(es, "vB", [128, 128])
            mdg = SB(es, "mdg", [128, 2, 4, 128], BF16)
            dma("pool", cmk[:], cmaskc[:, :, :, :], [], ["cmk"])
            dma("sp", vA[:], vmA[:, :], [], ["vA"])
            dma("sp", vB[:], vmB[:, :], [], ["vB"])
            dma("pool", mdg[:], mdiag[:, :, :, :], [], ["mdg"])
            ms("dve", VCX[:], 0.0, ["VCX"])
            ms("dve", VCX[:, :, :, 64:65], 1.0, ["VCX"])
            dma("pool", VCX[:, :, :, 65:129], ovx[:, :, :, :], [], ["VCX"])
            ms("dve", kcT[:], 0.0, ["kcT"])
            with ExitStack() as esc:
                kcm = SB(esc, "kcm", [128, 2, T], BF16)
                W1 = SB(esc, "W1", [128, 2, 32, 128], BF16)
                W2 = SB(esc, "W2", [128, 2, 64], BF16)
                peT = SB(esc, "peT", [128, 2, 32], BF16)
                hb = SB(esc, "hb", [128, 2])
                hg = SB(esc, "hg", [128, 256], BF16)
                dma("sp", kcm[:, 0, :], KCMP_d[:, :], [], ["kcm"])
                dma("sp", kcm[:, 1, :], VCMP_d[:, :], [], ["kcm"])
                for kv in range(2):
                    dma("pool", W1[:, kv, :, :], (wk1 if kv == 0 else wv1)[:, :, :], [], ["W1"])
                    dma("pool", peT[:, kv, :], (pekT if kv == 0 else pevT)[:, :], [], ["peT"])
                    dma("pool", W2[:, kv, :], (wk2 if kv == 0 else wv2)[:, :], [], ["W2"])
                for kv in range(2):
                    for j in range(32):
                        mm(ps[2][:, kv:kv + 1], W1[0:64, kv, j, :], peT[0:64, kv, j:j + 1], j == 0, j == 31, ["W1", "peT"], [PSK[2]])
                    cp("dve", hb[:, kv:kv + 1], ps[2][:, kv:kv + 1], [PSK[2]], ["hb"])
                for kv in range(2):
                    for g in range(2):
                        b = g
                        for j in range(32):
                            mm(ps[b][:, 0:255], W1[g * 64:(g + 1) * 64, kv, j, :], kcm[g * 64:(g + 1) * 64, kv, j:j + 16 * 254 + 1:16],
                               j == 0, j == 31, ["W1", "kcm"], [PSK[b]])
                        act(hg[:, 0:255], ps[b][:, 0:255], AF.Gelu_apprx_tanh, [PSK[b], "hb"], ["hg"], bias=hb[:, kv:kv + 1])
                        if kv == 0:
                            mm(ps[3][0:64, 0:255], W2[:, 0, :], hg[:, 0:255], True, True, ["W2", "hg"], [PSK[3]])
                            cp("dve", kcT[0:64, g, 0:255], ps[3][0:64, 0:255], [PSK[3]], ["kcT"])
                        else:
                            for nt in range(2):
                                rows = 128 if nt == 0 else 127
                                mm(ps[3][0:rows, nt * 64:(nt + 1) * 64], hg[:, nt * 128:nt * 128 + rows], W2[:, 1, :], True, True, ["W2", "hg"], [PSK[3]])
                                cp("dve", VCX[0:rows, nt, g, 0:64], ps[3][0:rows, nt * 64:(nt + 1) * 64], [PSK[3]], ["VCX"])
            P.barrier()
            if debug:
                d_kc = dout("d_kcT", [128, 2, 256], BF16)
                d_vcx = dout("d_VCX", [128, 2, 2, 130], BF16)
                dma("sp", d_kc[:, :, :], kcT[:], ["kcT"], [])
                dma("sp", d_vcx[:, :, :, :], VCX[:], ["VCX"], [])
                d_yn = dout("d_ynsa", [512, T], BF16)
                d_sel = dout("d_nsel", [T, 2, 64], BF16)
            QS = SB(es, "QS", [128, 3, 2, 4, 128], BF16)
            pT = SB(es, "pT", [128, 4, 512], BF16)
            O = SB(es, "O", [128, 2, 8, 64])
            Ob = SB(es, "Ob", [128, 512], BF16)
            ynT = SB(es, "ynT", [128, 4, CT], BF16)
            m2c = SB(es, "m2c", [128, 8, CT], BF16)
            mxa = SB(es, "mxa", [128, 8, CT], BF16)
            mxt = SB(es, "mxt", [128, CT])
            mixT = SB(es, "mixT", [128, 8, CT], BF16)
            xs = SB(es, "xs", [128, 4, D])
            hst = SB(es, "hst", [128, 4, D])
            Wbn = SB(es, "Wbn", [128, 4, D], BF16)
            Wout = SB(es, "Wout", [128, 8, D], BF16)
            sm = SB(es, "sm", [128, 96])
            imp = SB(es, "imp", [128, 64])
            scr = SB(es, "scr", [128, 64])
            wkk = SB(es, "wkk", [128, 64])
            nsb = SB(es, "nsb", [128, 2, 128], BF16)
            later = []
            dma("pool", Wbn[:], w_bn.rearrange("(k p) n -> p k n", p=128), [], ["Wbn"])
            dma("pool", Wout[:], w_out.rearrange("(k p) n -> p k n", p=128), [], ["Wout"])
            ms("dve", nsb[:], 0.0, ["nsb0", "nsb1"])
            ms("dve", QS[64:128], 0.0, ["QSn%d_%d" % (a_, b_) for a_ in range(3) for b_ in range(2)])
            state = {"sb": 0, "pt": 0}
            pendq = []
            SBK = (0, 1, 7)

            def s_tile(lhsT, rhs, rows, rk, mask, after):
                sb = SBK[state["sb"]]; state["sb"] = (state["sb"] + 1) % 3
                pi = state["pt"]; state["pt"] = (pi + 1) % 4
                mm(ps[sb][0:rows, :], lhsT, rhs, True, mask is None, rk, [PSK[sb]])
                if mask is not None:
                    mk, mkey = mask
                    mm(ps[sb][0:rows, :], idb[0:rows, 0:rows], mk, False, True, ["idb", mkey], [PSK[sb]])
                act(pT[0:rows, pi, :], ps[sb][0:rows, :], AF.Exp, [PSK[sb]], ["pT%d" % pi])
                pendq.append(lambda: after(pi))
                while len(pendq) > 2:
                    pendq.pop(0)()

            def pend_add(fn):
                if not pendq:
                    pendq.append(fn)
                    return
                prev = pendq[-1]

                def both():
                    prev()
                    fn()
                pendq[-1] = both

            def flush():
                while pendq:
                    pendq.pop(0)()

            def load_q(qb):
                buf = qb % 3
                dma("sp", QS[0:64, buf].rearrange("p g r t -> p (g r) t"), QT_d[:, :, qb * 128:(qb + 1) * 128].rearrange("h d t -> d h t"),
                    [], ["QSq%d" % buf])

            def emit_cmp(qb, g):
                buf = qb % 3
                obuf = qb % 2
                nts = [nt for nt in (0, 1) if qb - 16 * nt >= 0]
                gq = gates[:, qb, :]
                nbi = g

                def post():
                    ts("dve", sm[:, 0:2], ps[4][:, 64:64 + 130:129], 1e-30, ALU.add, [PSK[4]], ["sm_rs"])
                    ts("dve", sm[:, 2:4], ps[5][:, 64:64 + 130:129], 1e-30, ALU.add, [PSK[5]], ["sm_rs"])
                    P.op("dve", lambda e: e.reciprocal(sm[:, 0:4], sm[:, 0:4]), ["sm_rs"], ["sm_rs"])
                    for r in range(4):
                        bk = 4 + r // 2
                        o = (r % 2) * 129 + 65
                        if r == 0:
                            ts("dve", imp[:], ps[bk][:, o:o + 64], sm[:, 0:1], ALU.mult, [PSK[bk], "sm_rs"], ["imp"])
                        else:
                            stt(imp[:], ps[bk][:, o:o + 64], sm[:, r:r + 1], imp[:], ALU.mult, ALU.add, [PSK[bk], "sm_rs", "imp"], ["imp"])
                    tt("dve", sm[:, 4:8], sm[:, 0:4], gq[:, g * 4:g * 4 + 4], ALU.mult, ["sm_rs", "gates"], ["sm_cf"])
                    for r in range(4):
                        bk = 4 + r // 2
                        o = (r % 2) * 129
                        ts("dve", O[:, obuf, g * 4 + r, :], ps[bk][:, o:o + 64], sm[:, 4 + r:5 + r], ALU.mult, [PSK[bk], "sm_cf"], ["O%d_%d" % (obuf, g)])
                    c0 = 62 - 2 * qb
                    tt("dve", scr[:], imp[:], vA[:, c0:c0 + 64], ALU.mult, ["imp", "vA"], ["scr"])
                    tt("dve", scr[:], scr[:], vB[:, c0:c0 + 64], ALU.add, ["scr", "vB"], ["scr"])
                    ts("dve", scr[:, 0:1], scr[:, 0:1], 100.0, ALU.add, ["scr"], ["scr"])
                    P.op("dve", lambda e: e.max(out=sm[:, 32:40], in_=scr[:]), ["scr"], ["sm_m8"])
                    P.op("dve", lambda e: e.match_replace(out=wkk[:], in_to_replace=sm[:, 32:40], in_values=scr[:], imm_value=-1e30), ["scr", "sm_m8"], ["wkk"])
                    P.op("dve", lambda e: e.max(out=sm[:, 40:48], in_=wkk[:]), ["wkk"], ["sm_m8b"])
                    ts("dve", nsb[:, nbi, 64:128], scr[:], sm[:, 47:48], ALU.is_lt, ["scr", "sm_m8b"], ["nsb%d" % nbi], s2=NEGM, op1=ALU.mult)
                    if debug:
                        dma("sp", d_sel[qb * 128:(qb + 1) * 128, g, :], nsb[:, nbi, 64:128], ["nsb%d" % nbi], [])

                def post_pe():
                    pv = ps[6][:, 0:64].bitcast(BF16)
                    tr(pv, nsb[:, nbi, :], idb[:], ["nsb%d" % nbi, "idb"], [PSK[6]])
                    cp("dve", QS[64:128, buf, g], pv[64:128, :].unsqueeze(1).to_broadcast([64, 4, 128]), [PSK[6]], ["QSn%d_%d" % (buf, g)])
                later.append(post_pe)

                for nt in nts:
                    rows = 128 if nt == 0 else 127
                    dlt = qb - 16 * nt
                    mask = (cmk[0:rows, dlt].rearrange("p r t -> p (r t)"), "cmk") if dlt <= 16 else None

                    def after(pi, nt=nt, rows=rows):
                        for r in range(4):
                            bk = 4 + r // 2
                            o = (r % 2) * 129
                            mm(ps[bk][:, o:o + 129], pT[0:rows, pi, r * 128:(r + 1) * 128], VCX[0:rows, nt, g, 0:129],
                               nt == nts[0] and r % 2 == 0, (nt == nts[-1]) and r % 2 == 1, ["pT%d" % pi, "VCX"], [PSK[bk]])
                        if nt == nts[-1]:
                            post()
                    s_tile(kcT[:, g, nt * 128:nt * 128 + rows], QS[:, buf, g].rearrange("p r t -> p (r t)"), rows,
                           ["kcT", "QSq%d" % buf], mask, after)

            def finalize(bank, brn, qb, g):
                buf = qb % 2
                gq = gates[:, qb, :]
                k1, k2 = "sm_rs%d" % brn, "sm_cf%d" % brn
                o1 = 8 * brn
                ts("dve", sm[:, o1:o1 + 4], ps[bank][:, 64:64 + 65 * 3 + 1:65], 1e-30, ALU.add, [PSK[bank]], [k1])
                P.op("dve", lambda e: e.reciprocal(sm[:, o1:o1 + 4], sm[:, o1:o1 + 4]), [k1], [k1])
                tt("dve", sm[:, o1 + 4:o1 + 8], sm[:, o1:o1 + 4], gq[:, brn * 8 + g * 4:brn * 8 + g * 4 + 4], ALU.mult, [k1, "gates"], [k2])
                for r in range(4):
                    ok = "O%d_%d" % (buf, g)
                    stt(O[:, buf, g * 4 + r, :], ps[bank][:, r * 65:r * 65 + 64], sm[:, o1 + 4 + r:o1 + 5 + r], O[:, buf, g * 4 + r, :],
                        ALU.mult, ALU.add, [PSK[bank], k2, ok], [ok])

            def emit_selwin(qb, g):
                buf = qb % 3
                qk = ["QSq%d" % buf, "QSn%d_%d" % (buf, g)]
                rhs_full = QS[:, buf, g].rearrange("p r t -> p (r t)")
                rhs_q = QS[0:64, buf, g].rearrange("p r t -> p (r t)")
                for kt in range(qb + 1):
                    mask = (mdg[:, 0].rearrange("p r t -> p (r t)"), "mdg") if kt == qb else None

                    def after(pi, kt=kt):
                        for r in range(4):
                            mm(ps[2][:, r * 65:(r + 1) * 65], pT[:, pi, r * 128:(r + 1) * 128], Vs[:, kt, g, 0:65],
                               kt == 0 and r == 0, kt == qb and r == 3, ["pT%d" % pi, "Vs"], [PSK[2]])
                        if kt == qb:
                            finalize(2, 1, qb, g)
                    s_tile(KE[:, g, kt * 128:(kt + 1) * 128], rhs_full, 128, ["KE"] + qk, mask, after)
                k0 = max(0, qb - 4)
                for kt in range(k0, qb + 1):
                    mask = None
                    if kt == qb:
                        mask = (mdg[:, 0].rearrange("p r t -> p (r t)"), "mdg")
                    elif kt == qb - 4:
                        mask = (mdg[:, 1].rearrange("p r t -> p (r t)"), "mdg")

                    def after(pi, kt=kt):
                        for r in range(4):
                            mm(ps[3][:, r * 65:(r + 1) * 65], pT[:, pi, r * 128:(r + 1) * 128], Vw[:, kt, g, 0:65],
                               kt == k0 and r == 0, kt == qb and r == 3, ["pT%d" % pi, "Vw"], [PSK[3]])
                        if kt == qb:
                            finalize(3, 2, qb, g)
                    s_tile(kwT[:, g, kt * 128:(kt + 1) * 128], rhs_full, 128, ["kwT", "QSq%d" % buf], mask, after)

            def finish_qblock(qb):
                buf = qb % 2
                cp("pool", Ob[:], O[:, buf].rearrange("p h d -> p (h d)"), ["O%d_0" % buf, "O%d_1" % buf], ["Ob"])
                pv = ps[6][:, 0:256].bitcast(BF16)
                for f in range(4):
                    tr(pv[:, f * 128:(f + 1) * 128], Ob[:, f * 128:(f + 1) * 128], idb[:], ["Ob", "idb"], [PSK[6]])
                q4 = qb % 4
                cp("dve", ynT[:, :, q4 * 128:(q4 + 1) * 128], pv.rearrange("p (f t) -> p f t", f=4), [PSK[6]], ["ynT"])

            def project(c):
                t0 = c * CT
                dma("sp", m2c[:], M2_d[:, t0:t0 + CT].rearrange("(o p) t -> p o t", p=128), [], ["m2c"])
                dma("sp", mxa[:], MIXA_d[:, t0:t0 + CT].rearrange("(o p) t -> p o t", p=128), [], ["mxa"])
                dma("sp", xs[:], x[t0:t0 + CT, :].rearrange("(s p) d -> p s d", p=128), [], ["xs"])
                if debug:
                    dma("sp", d_yn[:, t0:t0 + CT].rearrange("(o p) t -> p o t", p=128), ynT[:], ["ynT"], [])
                for ot in range(8):
                    b = 6
                    for kt in range(4):
                        mm(ps[b][:, :], Wbn[:, kt, ot * 128:(ot + 1) * 128], ynT[:, kt, :], kt == 0, kt == 3, ["Wbn", "ynT"], [PSK[b]])
                    tt("dve", mxt[:], ps[b][:, :], m2c[:, ot, :], ALU.mult, [PSK[b], "m2c"], ["mxt"])
                    tt("dve", mixT[:, ot, :], mxt[:], mxa[:, ot, :], ALU.add, ["mxt", "mxa"], ["mixT%d" % ot])
                for s in range(4):
                    for nh in range(2):
                        b = 6
                        for kt in range(8):
                            mm(ps[b][:, :], mixT[:, kt, s * 128:(s + 1) * 128], Wout[:, kt, nh * 512:(nh + 1) * 512], kt == 0, kt == 7,
                               ["Wout", "mixT%d" % kt], [PSK[b]])
                        tt("dve", hst[:, s, nh * 512:(nh + 1) * 512], ps[b][:, :], xs[:, s, nh * 512:(nh + 1) * 512], ALU.add, [PSK[b], "xs"], ["hst"])
                dma("pool", H_d[t0:t0 + CT, :].rearrange("(s p) d -> p s d", p=128), hst[:], ["hst"], [])

            NI = 64
            load_q(0)
            for i in range(NI + 1):
                if i < NI:
                    qb, g = i // 2, i % 2
                    if g == 0 and qb + 1 < 32:
                        load_q(qb + 1)
                    emit_cmp(qb, g)
                if i >= 1:
                    qb, g = (i - 1) // 2, (i - 1) % 2
                    emit_selwin(qb, g)
                    if g == 1:
                        def fin(qb=qb):
                            finish_qblock(qb)
                            if qb % 4 == 3:
                                project(qb // 4)
                        pend_add(fin)
                if i == 0:
                    flush()
                while later:
                    later.pop(0)()
            flush()
        P.barrier()
        esKV.close()
        if stop_after == "B":
            if debug:
                dh = dout("d_H", [T, D])
                dma("sp", dh[:, :], H_d[:, :], [], [])
            P.finish()
            return nc, dbg

        def norm_T(gb, src, dstb, nT, ssq, gkey, tag=""):
            for s_ in range(4):
                act(dstb[:, s_, :], src[:, s_, :], AF.Square, [gkey], ["nb%s%d" % (tag, s_), "ssq"], accum_out=ssq[:, s_:s_ + 1])
            act(ssq[:, 4:8], ssq[:, 0:4], AF.Sqrt, ["ssq"], ["ssq2"], scale=1.0 / D, bias=EPS)
            P.op("dve", lambda e: e.reciprocal(ssq[:, 4:8], ssq[:, 4:8]), ["ssq2"], ["ssq2"])
            for s_ in range(4):
                stt(dstb[:, s_, :], src[:, s_, :], ssq[:, 4 + s_:5 + s_], gb[:], ALU.mult, ALU.mult, [gkey, "ssq2", "gb"], ["nb%s%d" % (tag, s_)])
            for kt in range(8):
                b = 2 + (kt % 2)
                pv = ps[b][:, 0:256].bitcast(BF16)
                for s_ in range(4):
                    tr(pv[:, s_ * 128:(s_ + 1) * 128], dstb[:, s_, kt * 128:(kt + 1) * 128], idb[:], ["nb%s%d" % (tag, s_), "idb"], [PSK[b]])
                if kt % 2 == 0:
                    act(nT[:, kt, :], pv, AF.Copy, [PSK[b]], ["nT%s%d" % (tag, kt)])
                else:
                    cp("dve", nT[:, kt, :], pv, [PSK[b]], ["nT%s%d" % (tag, kt)])

        with ExitStack() as es:
            Wup = SB(es, "Wup", [128, 8, 2 * DFF], BF16)
            gf = SB(es, "gf", [128, D])
            cw = SB(es, "cw", [128, 44, 3])
            cb = SB(es, "cb", [128, 44])
            dma("sp", gf[:], gffn[:, :], [], ["gb"])
            dma("sp", cw[:], cvw[:, :, :], [], ["cw"])
            dma("sp", cb[:], cvb[:, :], [], ["cw"])
            for q4 in range(4):
                for hf in range(2):
                    c0 = hf * DFF + q4 * 704
                    dma("pool", Wup[:, :, c0:c0 + 704], w_up.rearrange("(k p) n -> p k n", p=128)[:, :, c0:c0 + 704], [], ["Wup%d" % q4])
            hs = SB(es, "hs", [128, 4, D])
            hb_ = SB(es, "hb_", [128, 2, 4, D], BF16)
            ssq = SB(es, "ssq", [128, 8])
            n2T = SB(es, "n2T", [128, 2, 8, CT], BF16)
            he = SB(es, "he", [128, 3, CT + 2])
            acc = SB(es, "acc", [128, 3, CT])
            ga = SB(es, "ga", [128, CT])
            carry = SB(es, "carry", [128, 44, 2])
            actT = SB(es, "actT", [128, 22, CT], BF16)
            ms("dve", carry[:], 0.0, ["carry%d" % i for i in range(44)])
            def c1_pro(c):
                t0_ = c * CT
                dma("sp", hs[:], H_d[t0_:t0_ + CT, :].rearrange("(s p) d -> p s d", p=128), [], ["hs"])
                norm_T(gf, hs, hb_[:, c % 2], n2T[:, c % 2], ssq, "hs", tag="%d_" % (c % 2))

            c1_pro(0)
            for c in range(NCH):
                t0 = c * CT
                pbc = c % 2
                order = []
                for i in range(22):
                    order += [i, 22 + i]
                tail = [None]
                for oi, ft in enumerate(order):
                    b = (0, 1, 4, 5)[oi % 4]
                    par = oi % 3
                    wk_ = "Wup%d" % (((ft % 22) * 128) // 704)
                    wk2_ = "Wup%d" % (((ft % 22) * 128 + 127) // 704)
                    if oi == 20 and c + 1 < NCH:
                        c1_pro(c + 1)
                    for kt in range(8):
                        mm(ps[b][:, :], Wup[:, kt, ft * 128:(ft + 1) * 128], n2T[:, pbc, kt, :], kt == 0, kt == 7, [wk_, wk2_, "nT%d_%d" % (pbc, kt)], [PSK[b]])
                    hk, hck, ak = "he%d" % par, "hec%d" % par, "acc%d" % par
                    cp("dve", he[:, par, 0:2], carry[:, ft, :], ["carry%d" % ft], [hck])
                    act(he[:, par, 2:CT + 2], ps[b][:, :], AF.Copy, [PSK[b]], [hk])
                    act(acc[:, par, :], ps[b][:, :], AF.Identity, [PSK[b], "cw"], [ak], scale=cw[:, ft, 2:3], bias=cb[:, ft:ft + 1])
                    stt(acc[:, par, :], he[:, par, 1:CT + 1], cw[:, ft, 1:2], acc[:, par, :], ALU.mult, ALU.add, [hk, hck, ak, "cw"], [ak])
                    stt(acc[:, par, :], he[:, par, 0:CT], cw[:, ft, 0:1], acc[:, par, :], ALU.mult, ALU.add, [hk, hck, ak, "cw"], [ak])
                    cp("dve", carry[:, ft, :], he[:, par, CT:CT + 2], [hk], ["carry%d" % ft])
                    prev = tail[0]

                    def tl(ft=ft, par=par, ak=ak):
                        if ft < 22:
                            act(ga[:], acc[:, par, :], AF.Gelu_apprx_tanh, [ak], ["ga"])
                        else:
                            tt("dve", actT[:, ft - 22, :], ga[:], acc[:, par, :], ALU.mult, ["ga", ak], ["actT"])
                    tail[0] = tl
                    if prev is not None:
                        prev()
                tail[0]()
                dma("pool", ACT_d[:, t0:t0 + CT].rearrange("(o p) t -> p o t", p=128), actT[:], ["actT"], [])
        P.barrier()
        if stop_after == "C1":
            P.finish()
            return nc, dbg

        with ExitStack() as es:
            Wdn = SB(es, "Wdn", [128, 22, D], BF16)
            Wpg = SB(es, "Wpg", [128, 8, D], BF16)
            Wpp = SB(es, "Wpp", [128, 2, D], BF16)
            gfb = SB(es, "gfb", [128, D])
            gp = SB(es, "gp", [128, D])
            dma("sp", gp[:], gple[:, :], [], ["gb"])
            dma("sp", gfb[:], gfin[:, :], [], ["gfb"])
            for nh_ in range(2):
                dma("pool", Wdn[:, :, nh_ * 512:(nh_ + 1) * 512], w_down.rearrange("(k p) n -> p k n", p=128)[:, :, nh_ * 512:(nh_ + 1) * 512], [], ["Wdn%d" % nh_])
            dma("pool", Wpp[:], w_pp.rearrange("(k p) n -> p k n", p=128), [], ["Wpp"])
            dma("pool", Wpg[:], w_pg.rearrange("(k p) n -> p k n", p=128), [], ["Wpg"])
            aT = SB(es, "aT", [128, 2, 22, CT], BF16)
            hs2 = SB(es, "hs2", [128, 2, 4, D])
            h2b = SB(es, "h2b", [128, 4, D], BF16)
            ssq = SB(es, "ssq", [128, 12])
            n3T = SB(es, "n3T", [128, 8, CT], BF16)
            pin = SB(es, "pin", [128, 2, 4, 256])
            pb = SB(es, "pb", [128, 4, 256], BF16)
            ppT = SB(es, "ppT", [128, 2, CT], BF16)
            gsb = SB(es, "gsb", [128, CT])
            tmp = SB(es, "tmp", [128, CT])
            junk = SB(es, "junk", [128, D], BF16)

            def wdown(c):
                t0 = c * CT
                q = c % 2
                hk = "hs%d" % q
                dma("sp", aT[:, q], ACT_d[:, t0:t0 + CT].rearrange("(o p) t -> p o t", p=128), [], ["aT%d" % q])
                dma("sp", hs2[:, q], H_d[t0:t0 + CT, :].rearrange("(s p) d -> p s d", p=128), [], [hk])
                dma("sp", pin[:, q], pp[t0:t0 + CT, :].rearrange("(s p) d -> p s d", p=128), [], ["pin%d" % q])
                for s_ in range(4):
                    for nh in range(2):
                        b = 6 + (s_ * 2 + nh) % 2
                        for kt in range(22):
                            mm(ps[b][:, :], aT[:, q, kt, s_ * 128:(s_ + 1) * 128], Wdn[:, kt, nh * 512:(nh + 1) * 512], kt == 0, kt == 21,
                               ["aT%d" % q, "Wdn%d" % nh], [PSK[b]])
                        tt("dve", hs2[:, q, s_, nh * 512:(nh + 1) * 512], ps[b][:, :], hs2[:, q, s_, nh * 512:(nh + 1) * 512], ALU.add, [PSK[b], hk], [hk])

            def norm_a(src, dstb, gkey):
                for s_ in range(4):
                    act(dstb[:, s_, :], src[:, s_, :], AF.Square, [gkey], ["nb%d" % s_, "ssq"], accum_out=ssq[:, s_:s_ + 1])
                act(ssq[:, 4:8], ssq[:, 0:4], AF.Sqrt, ["ssq"], ["ssq2"], scale=1.0 / D, bias=EPS)
                P.op("dve", lambda e: e.reciprocal(ssq[:, 4:8], ssq[:, 4:8]), ["ssq2"], ["ssq2"])
                for s_ in range(4):
                    stt(dstb[:, s_, :], src[:, s_, :], ssq[:, 4 + s_:5 + s_], gp[:], ALU.mult, ALU.mult, [gkey, "ssq2", "gb"], ["nb%d" % s_])

            def norm_b(src, dstb, nT):
                for kt in range(8):
                    b = 2 + (kt % 2)
                    pv = ps[b][:, 0:256].bitcast(BF16)
                    for s_ in range(4):
                        tr(pv[:, s_ * 128:(s_ + 1) * 128], dstb[:, s_, kt * 128:(kt + 1) * 128], idb[:], ["nb%d" % s_, "idb"], [PSK[b]])
                    if kt % 2 == 0:
                        act(nT[:, kt, :], pv, AF.Copy, [PSK[b]], ["nT%d" % kt])
                    else:
                        cp("dve", nT[:, kt, :], pv, [PSK[b]], ["nT%d" % kt])

            def rest(c):
                t0 = c * CT
                q = c % 2
                hk = "hs%d" % q
                hs = hs2[:, q]
                norm_b(hs, h2b, n3T)
                cp("pool", pb[:], pin[:, q], ["pin%d" % q], ["pb"])
                for kt in range(2):
                    pv = ps[2][:, 0:256].bitcast(BF16)
                    for s_ in range(4):
                        tr(pv[:, s_ * 128:(s_ + 1) * 128], pb[:, s_, kt * 128:(kt + 1) * 128], idb[:], ["pb", "idb"], [PSK[2]])
                    cp("dve", ppT[:, kt, :], pv, [PSK[2]], ["ppT"])
                for s_ in range(4):
                    for nh in range(2):
                        b = (s_ * 2 + nh) % 2
                        for kt in range(8):
                            mm(ps[b][:, :], n3T[:, kt, s_ * 128:(s_ + 1) * 128], Wpg[:, kt, nh * 512:(nh + 1) * 512], kt == 0, kt == 7,
                               ["nT%d" % kt, "Wpg"], [PSK[b]])
                        act(gsb[:], ps[b][:, :], AF.Sigmoid, [PSK[b]], ["gsb"])
                        b2 = 4 + b
                        for kt in range(2):
                            mm(ps[b2][:, :], ppT[:, kt, s_ * 128:(s_ + 1) * 128], Wpp[:, kt, nh * 512:(nh + 1) * 512], kt == 0, kt == 1,
                               ["ppT", "Wpp"], [PSK[b2]])
                        tt("dve", tmp[:], ps[b2][:, :], gsb[:], ALU.mult, [PSK[b2], "gsb"], ["tmp"])
                        tt("dve", hs[:, s_, nh * 512:(nh + 1) * 512], hs[:, s_, nh * 512:(nh + 1) * 512], tmp[:], ALU.add, ["tmp", hk], [hk])
                for s_ in range(4):
                    act(junk[:], hs[:, s_, :], AF.Square, [hk], ["junk", "ssf"], accum_out=ssq[:, 8 + s_:9 + s_])
                act(ssq[:, 8:12], ssq[:, 8:12], AF.Sqrt, ["ssf"], ["ssf"], scale=1.0 / D, bias=EPS)
                P.op("dve", lambda e: e.reciprocal(ssq[:, 8:12], ssq[:, 8:12]), ["ssf"], ["ssf"])
                for s_ in range(4):
                    stt(hs[:, s_, :], hs[:, s_, :], ssq[:, 8 + s_:9 + s_], gfb[:], ALU.mult, ALU.mult, [hk, "ssf", "gfb"], [hk])
                dma("pool", out[t0:t0 + CT, :].rearrange("(s p) d -> p s d", p=128), hs, [hk], [])

            wdown(0)
            for c in range(NCH):
                norm_a(hs2[:, c % 2], h2b, "hs%d" % (c % 2))
                if c + 1 < NCH:
                    wdown(c + 1)
                rest(c)
        P.finish()
    return nc, dbg


def host_prep(inp, b):
    f = np.float32
    m = {}
    m["x"] = np.ascontiguousarray(inp["x"][b])
    m["p"] = np.ascontiguousarray(inp["p"][0, b])
    m["w_in"] = np.ascontiguousarray(inp["w_in"][0])
    for nm, k in (("gmix", "g_mix"), ("gffn", "g_ffn"), ("gple", "g_ple")):
        m[nm] = np.ascontiguousarray(np.broadcast_to(inp[k][0][None, :], (128, D)))
    m["gfin"] = np.ascontiguousarray(np.broadcast_to(inp["g_final"][None, :], (128, D)))

    def chl(a):
        return np.ascontiguousarray(a.reshape(16, 2, 64).transpose(1, 2, 0).reshape(128, 16))
    m["are"] = chl(inp["ssm_a_re"][0])
    m["aim"] = chl(inp["ssm_a_im"][0])
    m["ldt"] = chl(np.broadcast_to(inp["ssm_log_dt"][0][:, None], (32, 64)))
    bn_re = np.zeros((128, 16, 128), f); bn_im = np.zeros((128, 16, 128), f)
    cn_re = np.zeros((128, 16, 128), f); cn_im = np.zeros((128, 16, 128), f)
    for g in range(32):
        ct, gl = g // 2, g % 2
        o = (g % 8) * 16
        bn_re[gl * 64:(gl + 1) * 64, ct, o:o + 16] = inp["ssm_b_re"][0, g]
        bn_im[gl * 64:(gl + 1) * 64, ct, o:o + 16] = inp["ssm_b_im"][0, g]
        cn_re[o:o + 16, ct, gl * 64:(gl + 1) * 64] = inp["ssm_c_re"][0, g]
        cn_im[o:o + 16, ct, gl * 64:(gl + 1) * 64] = inp["ssm_c_im"][0, g]
    m["bnre"], m["bnim"], m["cnre"], m["cnim"] = bn_re, bn_im, cn_re, cn_im
    m["dsk"] = np.ascontiguousarray(inp["ssm_d"][0].reshape(4, 128).T)
    m["w_glu"] = np.ascontiguousarray(inp["ssm_w_glu"][0])
    m["w_bs"] = np.ascontiguousarray(inp["w_br_ssm"][0])
    m["w_bn"] = np.ascontiguousarray(inp["w_br_nsa"][0])
    m["w_out"] = np.ascontiguousarray(inp["w_out"][0])
    m["w_up"] = np.ascontiguousarray(inp["w_up"][0])
    m["cvw"] = np.ascontiguousarray(inp["conv_w"][0].T.reshape(44, 128, 3).transpose(1, 0, 2))
    m["cvb"] = np.ascontiguousarray(inp["conv_b"][0].reshape(44, 128).T)
    m["w_down"] = np.ascontiguousarray(inp["w_down"][0])
    m["w_pg"] = np.ascontiguousarray(inp["w_ple_gate"][0])
    m["w_pp"] = np.ascontiguousarray(inp["w_ple_proj"][0])
    def w1l(a):
        t_ = a.reshape(32, 64, 128).transpose(1, 0, 2)
        return np.ascontiguousarray(np.concatenate([t_, t_], axis=0))
    m["wk1"] = w1l(inp["cmp_wk1"][0]); m["wv1"] = w1l(inp["cmp_wv1"][0])
    m["wk2"] = np.ascontiguousarray(inp["cmp_wk2"][0]); m["wv2"] = np.ascontiguousarray(inp["cmp_wv2"][0])
    m["pekT"] = np.ascontiguousarray(np.concatenate([inp["cmp_pe_k"][0].T] * 2, axis=0))
    m["pevT"] = np.ascontiguousarray(np.concatenate([inp["cmp_pe_v"][0].T] * 2, axis=0))
    m["ident"] = np.eye(128, dtype=f)
    ek = np.zeros((64, T), f); ek[np.arange(T) // 64, np.arange(T)] = 1.0
    m["ekey"] = ek
    nl = np.arange(128)[:, None]; tl = np.arange(128)[None, :]
    cm = np.zeros((128, 17, 128), f)
    for dlt in range(17):
        cm[:, dlt, :] = np.where(16 * nl + 31 <= 128 * dlt + tl, 0.0, NEGM)
    m["cmaskc"] = np.ascontiguousarray(np.broadcast_to(cm[:, :, None, :], (128, 17, 4, 128)))
    tlp = np.arange(128)[:, None]; mm_ = np.arange(128)[None, :]
    valid = ((mm_ - 62) <= tlp // 64).astype(f)
    forced = (((mm_ - 62) == tlp // 64) | ((mm_ - 62) == tlp // 64 - 1)).astype(f)
    m["vmA"] = valid
    m["vmB"] = (valid - 1.0) + 100.0 * forced
    kk = np.arange(128)[:, None]; qq = np.arange(128)[None, :]
    md_ = np.stack([np.where(kk <= qq, 0.0, NEGM), np.where(kk > qq, 0.0, NEGM)], axis=1).astype(f)
    m["mdiag"] = np.ascontiguousarray(np.broadcast_to(md_[:, :, None, :], (128, 2, 4, 128)))
    starts = np.arange(256) * 16; ss_ = np.arange(64) * 64
    ov = ((starts[:, None] < ss_[None, :] + 64) & (starts[:, None] + 32 > ss_[None, :])).astype(f)
    ov[255] = 0
    ovl = ov.reshape(2, 128, 64).transpose(1, 0, 2)
    m["ovx"] = np.ascontiguousarray(np.broadcast_to(ovl[:, :, None, :], (128, 2, 2, 64)))
    return m


_CACHE = {}


def kernel(**inputs):
    inp = {k: np.asarray(v) for k, v in inputs.items()}
    if "nc" not in _CACHE:
        _CACHE["nc"] = build("C", False)[0]
    nc = _CACHE["nc"]
    in_maps = [host_prep(inp, b) for b in range(8)]
    res = run_bass_kernel_spmd(nc, in_maps, core_ids=list(range(8)))
    return np.stack([res.results[b]["out"] for b in range(8)], axis=0).astype(np.float32)
```

```python
import math
import numpy as np
import concourse.bass as bass
import concourse.mybir as mybir
from concourse.bass_utils import run_bass_kernel_spmd
from contextlib import ExitStack

F32 = mybir.dt.float32
BF16 = mybir.dt.bfloat16
AF = mybir.ActivationFunctionType
ALU = mybir.AluOpType
NDQ = 8

T = 4096
D = 1024
NCH = 8
CT = 512
IN_W = 3864
OFF_Q, OFF_KV, OFF_G, OFF_M = 512, 1024, 1792, 1816
DFF = 2816
EPS = 1e-6
LS = 256
NEGM = -30000.0


class Prog:
    def __init__(self, nc, es):
        self.nc = nc
        self.eng = {"pe": nc.tensor, "act": nc.scalar, "dve": nc.vector, "pool": nc.gpsimd, "sp": nc.sync}
        self.sems = {}
        for e in ("pe", "act", "dve", "pool"):
            self.sems[e] = es.enter_context(nc.semaphore("s_" + e))
        for q in ("sp", "pool", "act"):
            for i in range(NDQ):
                self.sems[(q, "d", i)] = es.enter_context(nc.semaphore("d_%s%d" % (q, i)))
        self.cnt = {e: 0 for e in self.eng}
        self.dcnt = {"sp": 0, "pool": 0, "act": 0}
        self.seen = {e: {} for e in self.eng}
        self.reg = {}
        self.last = {}

    def _wait(self, stream, toks, dma):
        e = self.eng[stream]
        for sk, v in toks:
            if sk == "pe" and stream == "pe" and not dma:
                continue
            if self.seen[stream].get(sk, 0) >= v:
                continue
            e.wait_ge(self.sems[sk], v)
            self.seen[stream][sk] = v

    def op(self, stream, fn, reads=(), writes=(), dma=False):
        toks = {}

        def add(t):
            if t is not None and toks.get(t[0], 0) < t[1]:
                toks[t[0]] = t[1]

        for k in reads:
            r = self.reg.get(k)
            if r is not None:
                add(r[0])
        for k in writes:
            r = self.reg.get(k)
            if r is not None:
                add(r[0])
                for sk, v in r[1].items():
                    add((sk, v))
        if dma:
            i = self.dcnt[stream]
            semkey = (stream, "d", i % NDQ)
            val = 16 * (i // NDQ + 1)
            if i >= NDQ:
                add((semkey, 16 * (i // NDQ)))
            self.dcnt[stream] = i + 1
        else:
            semkey = stream
            val = self.cnt[stream] + 1
            self.cnt[stream] = val
        self._wait(stream, sorted(toks.items(), key=lambda x: str(x[0])), dma)
        ins = fn(self.eng[stream])
        ins.then_inc(self.sems[semkey], 16 if dma else 1)
        tok = (semkey, val)
        for k in writes:
            self.reg[k] = [tok, {}]
        for k in reads:
            r = self.reg.setdefault(k, [None, {}])
            if r[1].get(semkey, 0) < val:
                r[1][semkey] = val
        self.last[semkey] = val
        return tok

    def barrier(self):
        for s in self.eng:
            self._wait(s, sorted(self.last.items(), key=lambda x: str(x[0])), True)
        self.reg = {}

    def finish(self):
        self._wait("sp", sorted(self.last.items(), key=lambda x: str(x[0])), True)


def build(stop_after="C", debug=False):
    nc = bass.Bass("TRN2", target_bir_lowering=False)

    def din(name, shape, dt=F32):
        return nc.dram_tensor(name, list(shape), dt, kind="ExternalInput").ap()

    def dscr(name, shape, dt):
        return nc.dram_tensor(name, list(shape), dt).ap()

    x = din("x", [T, D])
    pp = din("p", [T, 256])
    w_in = din("w_in", [D, IN_W])
    gmix = din("gmix", [128, D])
    gffn = din("gffn", [128, D])
    gple = din("gple", [128, D])
    gfin = din("gfin", [128, D])
    are = din("are", [128, 16])
    aim = din("aim", [128, 16])
    ldt = din("ldt", [128, 16])
    bnre = din("bnre", [128, 16, 128])
    bnim = din("bnim", [128, 16, 128])
    cnre = din("cnre", [128, 16, 128])
    cnim = din("cnim", [128, 16, 128])
    dsk = din("dsk", [128, 4])
    w_glu = din("w_glu", [512, 512])
    w_bs = din("w_bs", [512, D])
    w_bn = din("w_bn", [512, D])
    w_out = din("w_out", [D, D])
    w_up = din("w_up", [D, 2 * DFF])
    cvw = din("cvw", [128, 44, 3])
    cvb = din("cvb", [128, 44])
    w_down = din("w_down", [DFF, D])
    w_pg = din("w_pg", [D, D])
    w_pp = din("w_pp", [256, D])
    wk1 = din("wk1", [128, 32, 128])
    wv1 = din("wv1", [128, 32, 128])
    wk2 = din("wk2", [128, 64])
    wv2 = din("wv2", [128, 64])
    pekT = din("pekT", [128, 32])
    pevT = din("pevT", [128, 32])
    ident = din("ident", [128, 128])
    ekey = din("ekey", [64, T])
    cmaskc = din("cmaskc", [128, 17, 4, 128])
    vmA = din("vmA", [128, 128])
    vmB = din("vmB", [128, 128])
    mdiag = din("mdiag", [128, 2, 4, 128])
    ovx = din("ovx", [128, 2, 2, 64])

    out = nc.dram_tensor("out", [T, D], F32, kind="ExternalOutput").ap()
    dbg = {}

    def dout(name, shape, dt=F32):
        dbg[name] = nc.dram_tensor(name, list(shape), dt, kind="ExternalOutput").ap()
        return dbg[name]

    QT_d = dscr("QT_d", [8, 64, T], BF16)
    KCMP_d = dscr("KCMP_d", [128, T], BF16)
    VCMP_d = dscr("VCMP_d", [128, T], BF16)
    KSEL_d = dscr("KSEL_d", [2, 64, T], BF16)
    KWIN_d = dscr("KWIN_d", [2, 64, T], BF16)
    VSEL_d = dscr("VSEL_d", [T, 128], BF16)
    VWIN_d = dscr("VWIN_d", [T, 128], BF16)
    M2_d = dscr("M2_d", [D, T], BF16)
    M1_d = dscr("M1_d", [D, T], BF16)
    U_d = dscr("U_d", [512, T], F32)
    MIXA_d = dscr("MIXA_d", [D, T], BF16)
    H_d = dscr("H_d", [T, D], F32)
    ACT_d = dscr("ACT_d", [DFF, T], BF16)

    with ExitStack() as es0:
        P = Prog(nc, es0)

        def dma(q, out_, in_, r, w):
            return P.op(q, lambda e: e.dma_start(out=out_, in_=in_), r, w, dma=True)

        def act(out_, in_, func, r, w, **kw):
            return P.op("act", lambda e: e.activation(out=out_, in_=in_, func=func, **kw), r, w)

        def mm(out_, lhsT, rhs, start, stop, r, w):
            return P.op("pe", lambda e: e.matmul(out_, lhsT, rhs, start=start, stop=stop), r, w)

        def tr(out_, in_, idn, r, w):
            return P.op("pe", lambda e: e.transpose(out=out_, in_=in_, identity=idn), r, w)

        def tt(eng, out_, in0, in1, op, r, w):
            return P.op(eng, lambda e: e.tensor_tensor(out=out_, in0=in0, in1=in1, op=op), r, w)

        def ts(eng, out_, in0, s1, op0, r, w, s2=None, op1=None):
            if op1 is None:
                return P.op(eng, lambda e: e.tensor_scalar(out=out_, in0=in0, scalar1=s1, scalar2=None, op0=op0), r, w)
            return P.op(eng, lambda e: e.tensor_scalar(out=out_, in0=in0, scalar1=s1, scalar2=s2, op0=op0, op1=op1), r, w)

        def stt(out_, in0, sc, in1, op0, op1, r, w):
            return P.op("dve", lambda e: e.scalar_tensor_tensor(out=out_, in0=in0, scalar=sc, in1=in1, op0=op0, op1=op1), r, w)

        def cp(eng, out_, in_, r, w):
            return P.op(eng, lambda e: e.tensor_copy(out_, in_), r, w)

        def ms(eng, ap, val, w):
            return P.op(eng, lambda e: e.memset(ap, val), (), w)

        nmc = [0]

        def SB(es, name, shape, dt=F32):
            nmc[0] += 1
            return es.enter_context(nc.sbuf_tensor("%s_%d" % (name, nmc[0]), list(shape), dt))

        ps = [es0.enter_context(nc.psum_tensor("ps%d" % i, [128, 512], F32)) for i in range(8)]
        PSK = ["ps%d" % i for i in range(8)]

        idf = SB(es0, "idf", [128, 128])
        idb = SB(es0, "idb", [128, 128], BF16)
        gates = SB(es0, "gates", [128, 32, 24])
        dma("sp", idf[:], ident[:, :], [], ["idf"])
        dma("pool", idb[:], ident[:, :], [], ["idb"])

        with ExitStack() as es:
            Win = SB(es, "Win", [128, 8, IN_W], BF16)
            gm = SB(es, "gm", [128, D])
            dma("sp", gm[:], gmix[:, :], [], ["gm"])
            WB = [0, 1024, 1816, 2840, 3864]
            for bi in range(4):
                dma("pool", Win[:, :, WB[bi]:WB[bi + 1]], w_in.rearrange("(k p) n -> p k n", p=128)[:, :, WB[bi]:WB[bi + 1]], [], ["Win%d" % bi])

            def wkey(c0):
                return "Win%d" % max(i for i in range(4) if WB[i] <= c0)
            xs = SB(es, "xs", [128, 4, D])
            xb = SB(es, "xb", [128, 2, 4, D], BF16)
            ss = SB(es, "ss", [128, 8])
            n1T = SB(es, "n1T", [128, 2, 8, CT], BF16)
            ust = SB(es, "ust", [128, 4, CT])
            qst = SB(es, "qst", [128, 4, CT], BF16)
            kvst = SB(es, "kvst", [128, 2, CT], BF16)
            k2st = SB(es, "k2st", [128, 2, CT], BF16)
            vtmp = SB(es, "vtmp", [128, 2, CT], BF16)
            vst = SB(es, "vst", [128, 2, 4, 128], BF16)
            gT = SB(es, "gT", [24, CT])
            mst = SB(es, "mst", [128, 2, 8, CT], BF16)

            tiles = []
            for i in range(4):
                tiles.append((i * 128, 128, "u", i))
            for h in range(4):
                tiles.append((OFF_Q + h * 128, 128, "q", h))
            tiles.append((OFF_KV + 0, 128, "kcmp", 0))
            tiles.append((OFF_KV + 128, 128, "vcmp", 0))
            tiles.append((OFF_KV + 256, 128, "ksel", 0))
            tiles.append((OFF_KV + 384, 128, "vsel", 0))
            tiles.append((OFF_KV + 512, 128, "kwin", 0))
            tiles.append((OFF_KV + 640, 128, "vwin", 0))
            tiles.append((OFF_G, 24, "gate", 0))
            for j in range(16):
                tiles.append((OFF_M + j * 128, 128, "m", j))

            def prologue(c):
                t0_ = c * CT
                pb_ = c % 2
                dma("sp", xs[:], x[t0_:t0_ + CT, :].rearrange("(s p) d -> p s d", p=128), [], ["xs"])
                for s in range(4):
                    act(xb[:, pb_, s, :], xs[:, s, :], AF.Square, ["xs"], ["xb%d_%d" % (pb_, s), "ss"], accum_out=ss[:, s:s + 1])
                act(ss[:, 4:8], ss[:, 0:4], AF.Sqrt, ["ss"], ["ss2"], scale=1.0 / D, bias=EPS)
                P.op("dve", lambda e: e.reciprocal(ss[:, 4:8], ss[:, 4:8]), ["ss2"], ["ss2"])
                for s in range(4):
                    stt(xb[:, pb_, s, :], xs[:, s, :], ss[:, 4 + s:5 + s], gm[:], ALU.mult, ALU.mult, ["xs", "ss2", "gm"], ["xb%d_%d" % (pb_, s)])
                for kt in range(8):
                    b = 2 + (kt % 2)
                    pv = ps[b][:, 0:256].bitcast(BF16)
                    for s in range(4):
                        tr(pv[:, s * 128:(s + 1) * 128], xb[:, pb_, s, kt * 128:(kt + 1) * 128], idb[:], ["xb%d_%d" % (pb_, s), "idb"], [PSK[b]])
                    if kt % 2 == 0:
                        act(n1T[:, pb_, kt, :], pv, AF.Copy, [PSK[b]], ["n1T%d_%d" % (pb_, kt)])
                    else:
                        cp("dve", n1T[:, pb_, kt, :], pv, [PSK[b]], ["n1T%d_%d" % (pb_, kt)])

            prologue(0)
            for c in range(NCH):
                t0 = c * CT
                pbc = c % 2
                for ti, (c0, wd, kind, idx) in enumerate(tiles):
                    b = (0, 1, 6, 7)[ti % 4]
                    if ti == 12 and c + 1 < NCH:
                        prologue(c + 1)
                    for kt in range(8):
                        mm(ps[b][0:wd, :], Win[:, kt, c0:c0 + wd], n1T[:, pbc, kt, :], kt == 0, kt == 7,
                           [wkey(c0), "n1T%d_%d" % (pbc, kt)], [PSK[b]])
                    src = ps[b][0:wd, :]
                    if kind == "u":
                        cp("dve", ust[:, idx, :], src, [PSK[b]], ["ust"])
                        if idx == 3:
                            dma("pool", U_d[:, t0:t0 + CT].rearrange("(o p) t -> p o t", p=128), ust[:], ["ust"], [])
                    elif kind == "q":
                        act(qst[:, idx, :], src, AF.Copy, [PSK[b]], ["qst"], scale=0.125)
                        if idx == 3:
                            for hh in range(2):
                                dma("pool", QT_d[hh::2, :, t0:t0 + CT].rearrange("h d t -> d h t"), qst[hh * 64:(hh + 1) * 64], ["qst"], [])
                    elif kind in ("kcmp", "vcmp"):
                        j = 0 if kind == "kcmp" else 1
                        act(kvst[:, j, :], src, AF.Copy, [PSK[b]], ["kvst%d" % j])
                        dma("pool", (KCMP_d if j == 0 else VCMP_d)[:, t0:t0 + CT], kvst[:, j, :], ["kvst%d" % j], [])
                    elif kind in ("ksel", "kwin"):
                        j = 0 if kind == "ksel" else 1
                        cp("dve", k2st[:, j, :], src, [PSK[b]], ["k2st%d" % j])
                        for gg in range(2):
                            dma("pool", (KSEL_d if kind == "ksel" else KWIN_d)[gg, :, t0:t0 + CT], k2st[gg * 64:(gg + 1) * 64, j, :], ["k2st%d" % j], [])
                    elif kind in ("vsel", "vwin"):
                        j = 0 if kind == "vsel" else 1
                        act(vtmp[:, j, :], src, AF.Copy, [PSK[b]], ["vtmp%d" % j])
                        pv = ps[4][:, 0:256].bitcast(BF16)
                        for s in range(4):
                            tr(pv[:, s * 128:(s + 1) * 128], vtmp[:, j, s * 128:(s + 1) * 128], idb[:], ["vtmp%d" % j, "idb"], [PSK[4]])
                        cp("dve", vst[:, j].rearrange("p s c -> p (s c)"), pv, [PSK[4]], ["vst%d" % j])
                        dma("pool", (VSEL_d if j == 0 else VWIN_d)[t0:t0 + CT, :].rearrange("(s p) c -> p s c", p=128), vst[:, j], ["vst%d" % j], [])
                    elif kind == "gate":
                        act(gT[:, :], src, AF.Sigmoid, [PSK[b]], ["gT"])
                        for s in range(4):
                            tr(ps[5][:, s * 24:(s + 1) * 24], gT[:, s * 128:(s + 1) * 128], idf[0:24, 0:24], ["gT", "idf"], [PSK[5]])
                        cp("dve", gates[:, c * 4:(c + 1) * 4, :].rearrange("p s g -> p (s g)"), ps[5][:, 0:96], [PSK[5]], ["gates"])
                    else:
                        h2 = idx // 8
                        act(mst[:, h2, idx % 8, :], src, AF.Sigmoid, [PSK[b]], ["mst%d" % h2])
                        if idx % 8 == 7:
                            dma("pool", (M1_d if h2 == 0 else M2_d)[:, t0:t0 + CT].rearrange("(o p) t -> p o t", p=128), mst[:, h2], ["mst%d" % h2], [])
        P.barrier()
        if stop_after == "A1":
            P.finish()
            return nc, dbg

        esKV = ExitStack()
        KE = SB(esKV, "KE", [128, 2, T], BF16)
        Vs = SB(esKV, "Vs", [128, 32, 2, 66], BF16)
        for g in range(2):
            dma("pool", KE[64:128, g, :], ekey[:, :], [], ["KE"])
            dma("sp", KE[0:64, g, :], KSEL_d[g, :, :], [], ["KE"])
        ms("dve", Vs[:, :, :, 64:66], 1.0, ["Vs"])
        for g in range(2):
            dma("sp", Vs[:, :, g, 0:64], VSEL_d.rearrange("(k p) c -> p k c", p=128)[:, :, g * 64:(g + 1) * 64], [], ["Vs"])

        with ExitStack() as es:
            Ec = SB(es, "Ec", [128, 16, LS])
            Es = SB(es, "Es", [128, 16, LS])
            rr = SB(es, "rr", [128, 16])
            BT = SB(es, "BT", [128, 2, 16, 128], BF16)
            CTw = SB(es, "CTw", [128, 3, 16, 128], BF16)
            dskip = SB(es, "dskip", [128, 4])
            Wglu = SB(es, "Wglu", [128, 4, 512], BF16)
            Wbs = SB(es, "Wbs", [128, 4, D], BF16)
            dma("sp", dskip[:], dsk[:, :], [], ["dskip"])
            dma("pool", Wglu[:], w_glu.rearrange("(k p) n -> p k n", p=128), [], ["Wglu"])
            dma("pool", Wbs[:], w_bs.rearrange("(k p) n -> p k n", p=128), [], ["Wbs"])
            with ExitStack() as esp:
                v16 = SB(esp, "v16", [128, 24, 16])
                V = lambda i: v16[:, i, :]
                Bn = SB(esp, "Bn", [128, 2, 16, 128])
                Bw = SB(esp, "Bw", [128, 3, 16, 128])
                tmpE = SB(esp, "tmpE", [128, 2, 16, LS // 2])
                hp = SB(esp, "hp", [128, 1])
                dma("sp", V(0), are[:, :], [], ["v0"])
                dma("sp", V(1), aim[:, :], [], ["v1"])
                dma("sp", V(2), ldt[:, :], [], ["v2"])
                dma("sp", Bn[:, 0], bnre[:, :, :], [], ["Bn0"])
                dma("sp", Bn[:, 1], bnim[:, :, :], [], ["Bn1"])
                ts("dve", V(3), V(0), -1e-4, ALU.min, ["v0"], ["v3"])
                act(V(4), V(2), AF.Exp, ["v2"], ["v4"])
                tt("dve", V(5), V(3), V(4), ALU.mult, ["v3", "v4"], ["v5"])
                act(rr[:], V(5), AF.Exp, ["v5"], ["rr"])
                tt("dve", V(6), V(1), V(4), ALU.mult, ["v1", "v4"], ["v6"])
                ms("dve", hp[:], math.pi / 2, ["hp"])
                act(V(7), V(6), AF.Sin, ["v6", "hp"], ["v7"], scale=1.0 / 32, bias=hp[:, 0:1])
                act(V(8), V(6), AF.Sin, ["v6"], ["v8"], scale=1.0 / 32)
                for _ in range(5):
                    tt("dve", V(9), V(7), V(7), ALU.mult, ["v7"], ["v9"])
                    tt("dve", V(10), V(8), V(8), ALU.mult, ["v8"], ["v10"])
                    tt("dve", V(11), V(7), V(8), ALU.mult, ["v7", "v8"], ["v11"])
                    tt("dve", V(7), V(9), V(10), ALU.subtract, ["v9", "v10"], ["v7"])
                    ts("dve", V(8), V(11), 2.0, ALU.mult, ["v11"], ["v8"])
                tt("dve", V(12), rr[:], V(7), ALU.mult, ["rr", "v7"], ["v12"])
                ts("dve", V(12), V(12), -1.0, ALU.add, ["v12"], ["v12"])
                tt("dve", V(13), rr[:], V(8), ALU.mult, ["rr", "v8"], ["v13"])
                tt("dve", V(14), V(3), V(3), ALU.mult, ["v3"], ["v14"])
                tt("dve", V(15), V(1), V(1), ALU.mult, ["v1"], ["v15"])
                tt("dve", V(14), V(14), V(15), ALU.add, ["v14", "v15"], ["v14"])
                P.op("dve", lambda e: e.reciprocal(V(14), V(14)), ["v14"], ["v14"])
                tt("dve", V(16), V(12), V(3), ALU.mult, ["v12", "v3"], ["v16"])
                tt("dve", V(18), V(13), V(1), ALU.mult, ["v13", "v1"], ["v18"])
                tt("dve", V(16), V(16), V(18), ALU.add, ["v16", "v18"], ["v16"])
                tt("dve", V(16), V(16), V(14), ALU.mult, ["v16", "v14"], ["v16"])
                tt("dve", V(17), V(13), V(3), ALU.mult, ["v13", "v3"], ["v17"])
                tt("dve", V(18), V(12), V(1), ALU.mult, ["v12", "v1", "v16"], ["v18"])
                tt("dve", V(17), V(17), V(18), ALU.subtract, ["v17", "v18"], ["v17"])
                tt("dve", V(17), V(17), V(14), ALU.mult, ["v17", "v14"], ["v17"])
                cp("dve", Ec[:, :, 0:1], V(7).unsqueeze(2), ["v7"], ["Ec"])
                cp("dve", Es[:, :, 0:1], V(8).unsqueeze(2), ["v8"], ["Es"])
                cp("dve", V(19), V(7), ["v7"], ["v19"])
                cp("dve", V(20), V(8), ["v8"], ["v20"])
                n = 1
                while n < LS:
                    cb = V(19).unsqueeze(2).to_broadcast([128, 16, n])
                    sb_ = V(20).unsqueeze(2).to_broadcast([128, 16, n])
                    tt("dve", tmpE[:, 0, :, 0:n], Ec[:, :, 0:n], cb, ALU.mult, ["Ec", "v19"], ["tE0"])
                    tt("dve", tmpE[:, 1, :, 0:n], Es[:, :, 0:n], sb_, ALU.mult, ["Es", "v20"], ["tE1"])
                    tt("dve", Ec[:, :, n:2 * n], tmpE[:, 0, :, 0:n], tmpE[:, 1, :, 0:n], ALU.subtract, ["tE0", "tE1", "Ec"], ["Ec"])
                    tt("dve", tmpE[:, 0, :, 0:n], Ec[:, :, 0:n], sb_, ALU.mult, ["Ec", "v20"], ["tE0"])
                    tt("dve", tmpE[:, 1, :, 0:n], Es[:, :, 0:n], cb, ALU.mult, ["Es", "v19"], ["tE1"])
                    tt("dve", Es[:, :, n:2 * n], tmpE[:, 0, :, 0:n], tmpE[:, 1, :, 0:n], ALU.add, ["tE0", "tE1", "Es"], ["Es"])
                    tt("dve", V(9), V(19), V(19), ALU.mult, ["v19"], ["v9"])
                    tt("dve", V(10), V(20), V(20), ALU.mult, ["v20"], ["v10"])
                    tt("dve", V(11), V(19), V(20), ALU.mult, ["v19", "v20"], ["v11"])
                    tt("dve", V(19), V(9), V(10), ALU.subtract, ["v9", "v10"], ["v19"])
                    ts("dve", V(20), V(11), 2.0, ALU.mult, ["v11"], ["v20"])
                    n *= 2
                frb = V(16).unsqueeze(2).to_broadcast([128, 16, 128])
                fib = V(17).unsqueeze(2).to_broadcast([128, 16, 128])
                tt("pool", Bw[:, 0], Bn[:, 1], fib, ALU.mult, ["Bn1", "v17"], ["Bw0"])
                tt("pool", Bw[:, 1], Bn[:, 0], frb, ALU.mult, ["Bn0", "v16"], ["Bw1"])
                tt("pool", Bw[:, 1], Bw[:, 1], Bw[:, 0], ALU.subtract, ["Bw0", "Bw1"], ["Bw1"])
                tt("pool", Bw[:, 0], Bn[:, 1], frb, ALU.mult, ["Bn1", "v16", "Bw1"], ["Bw0"])
                tt("pool", Bw[:, 2], Bn[:, 0], fib, ALU.mult, ["Bn0", "v17"], ["Bw2"])
                tt("pool", Bw[:, 2], Bw[:, 2], Bw[:, 0], ALU.add, ["Bw0", "Bw2"], ["Bw2"])
                for ri in range(2):
                    for ct in range(16):
                        b = 2 + (ct % 2)
                        tr(ps[b][:, 0:128], Bw[:, 1 + ri, ct, :], idf[:], ["Bw%d" % (1 + ri), "idf"], [PSK[b]])
                        act(BT[:, ri, ct, :], ps[b][:, 0:128], AF.Copy, [PSK[b]], ["BT"])
                dma("sp", Bn[:, 0], cnre[:, :, :], ["Bn0"], ["Bn0"])
                dma("sp", Bn[:, 1], cnim[:, :, :], ["Bn1"], ["Bn1"])
                for ri in range(2):
                    for ct in range(16):
                        b = 2 + (ct % 2)
                        tr(ps[b][:, 0:128], Bn[:, ri, ct, :], idf[:], ["Bn%d" % ri, "idf"], [PSK[b]])
                        act(CTw[:, ri, ct, :], ps[b][:, 0:128], AF.Copy, [PSK[b]], ["CT"], scale=(1.0 if ri == 0 else -1.0))
                        if ri == 0:
                            act(CTw[:, 2, ct, :], ps[b][:, 0:128], AF.Copy, [PSK[b]], ["CT"], scale=-1.0)
            P.barrier()
            EEb = SB(es, "EEb", [128, 16, 3, LS], BF16)
            EsN = SB(es, "EsN", [128, 16, 2, LS], BF16)
            act(EEb[:, :, 0, :], Ec[:], AF.Copy, ["Ec"], ["EEb"])
            act(EEb[:, :, 1, :], Es[:], AF.Copy, ["Es"], ["EEb"])
            act(EEb[:, :, 2, :], Ec[:], AF.Copy, ["Ec"], ["EEb"])
            act(EsN[:, :, 0, :], Es[:], AF.Copy, ["Es"], ["EsN"])
            act(EsN[:, :, 1, :], Es[:], AF.Copy, ["Es"], ["EsN"], scale=-1.0)
            bub = SB(es, "bub", [128, 2, 4, LS], BF16)
            uTf = SB(es, "uTf", [128, 4, CT])
            uTb = SB(es, "uTb", [128, 4, CT], BF16)
            m1T = SB(es, "m1T", [128, 8, CT], BF16)
            sw = SB(es, "sw", [128, 2, 1, 8])
            swb = SB(es, "swb", [128, 2, 8, LS], BF16)
            sbf = SB(es, "sbf", [128, 2, 4, LS], BF16)
            stc = SB(es, "stc", [128, 2, 16])
            wl = SB(es, "wl", [128, 2, 16])
            wt = SB(es, "wt", [128, 4, 16])
            ysb = SB(es, "ysb", [128, LS])
            yg = SB(es, "yg", [128, 4, CT])
            ygb = SB(es, "ygb", [128, 4, CT], BF16)
            sg = SB(es, "sg", [128, CT])
            yssm = SB(es, "yssm", [128, 4, CT], BF16)
            mixst = SB(es, "mixst", [128, 8, CT], BF16)
            ms("dve", stc[:], 0.0, ["stc"])
            if debug:
                d_y = dout("d_yssm", [512, T], BF16)
            for c in range(NCH):
                t0 = c * CT
                dma("sp", uTf[:], U_d[:, t0:t0 + CT].rearrange("(o p) t -> p o t", p=128), [], ["uTf%d" % i for i in range(4)])
                dma("sp", m1T[:], M1_d[:, t0:t0 + CT].rearrange("(o p) t -> p o t", p=128), [], ["m1T"])
                for i in range(4):
                    act(uTb[:, i, :], uTf[:, i, :], AF.Copy, ["uTf%d" % i], ["uTb%d" % i])
                for sc in range(CT // LS):
                    cs = slice(sc * LS, (sc + 1) * LS)

                    def bu(ct):
                        b_ = 4 + (ct % 2)
                        mm(ps[b_][:, 0:LS], BT[:, 0, ct, :], uTb[:, ct // 4, cs], True, True, ["BT", "uTb%d" % (ct // 4)], [PSK[b_]])
                        mm(ps[b_][:, LS:2 * LS], BT[:, 1, ct, :], uTb[:, ct // 4, cs], True, True, ["BT", "uTb%d" % (ct // 4)], [PSK[b_]])
                    def evac(ct):
                        b_ = 4 + (ct % 2)
                        p_ = ct % 2
                        act(bub[:, p_, 0:2, :].rearrange("p a t -> p (a t)"), ps[b_][:, 0:2 * LS], AF.Copy, [PSK[b_]], ["bub%d" % p_])
                        act(bub[:, p_, 2, :], ps[b_][:, LS:2 * LS], AF.Copy, [PSK[b_]], ["bub%d" % p_])
                        act(bub[:, p_, 3, :], ps[b_][:, 0:LS], AF.Copy, [PSK[b_]], ["bub%d" % p_])
                    bu(0)
                    bu(1)
                    evac(0)
                    bu(2)
                    for ct in range(16):
                        b = 4 + (ct % 2)
                        yt = ct // 4
                        par = ct % 2
                        bre = ps[b][:, 0:LS]
                        bim = ps[b][:, LS:2 * LS]
                        W = lambda i: sw[:, par, i, :]
                        K = lambda i: "sw%d_%d" % (par, i)
                        Wb = lambda i: swb[:, par, i, :]
                        Kb = lambda i: "swb%d_%d" % (par, i)
                        bk_ = "bub%d" % par
                        tt("dve", swb[:, par, 0:2, :], bub[:, par, 0:2, :], EEb[:, ct, 0:3:2, :], ALU.mult, [bk_, "EEb"], ["X%d" % par])
                        tt("dve", swb[:, par, 2:4, :], bub[:, par, 2:4, :], EsN[:, ct, :, :], ALU.mult, [bk_, "EsN"], ["Y%d" % par])
                        tt("dve", swb[:, par, 4:6, :], swb[:, par, 0:2, :], swb[:, par, 2:4, :], ALU.add, ["X%d" % par, "Y%d" % par], ["M%d" % par])
                        if ct + 1 < 16:
                            evac(ct + 1)
                            if ct + 3 < 16:
                                bu(ct + 3)
                        rb = rr[:, ct:ct + 1].to_broadcast([128, LS])
                        P.op("dve", lambda e: e.tensor_tensor_scan(out=Wb(7), data0=rb, data1=Wb(5), initial=stc[:, 1, ct:ct + 1], op0=ALU.mult, op1=ALU.add),
                             ["M%d" % par, "rr", "stc"], [Kb(7)])
                        P.op("dve", lambda e: e.tensor_tensor_scan(out=Wb(6), data0=rb, data1=Wb(4), initial=stc[:, 0, ct:ct + 1], op0=ALU.mult, op1=ALU.add),
                             ["M%d" % par, "rr", "stc"], [Kb(6)])
                        act(wl[:, 0, ct:ct + 1], swb[:, par, 6, LS - 1:LS], AF.Copy, [Kb(6)], ["wl%d" % ct])
                        act(wl[:, 1, ct:ct + 1], swb[:, par, 7, LS - 1:LS], AF.Copy, [Kb(7)], ["wl%d" % ct])
                        Q = lambda i: sbf[:, par, i, :]
                        QK = lambda i: "sbf%d_%d" % (par, i // 2)
                        tt("dve", sbf[:, par, 0:2, :], swb[:, par, 6:8, :], EEb[:, ct, 0:2, :], ALU.mult, [Kb(6), Kb(7), "EEb"], [QK(0)])
                        tt("dve", sbf[:, par, 2:4, :], swb[:, par, 6:8, :], EEb[:, ct, 1:3, :], ALU.mult, [Kb(6), Kb(7), "EEb"], [QK(2)])
                        yb = 6 + (yt % 2)
                        mm(ps[yb][:, 0:LS], CTw[:, 0, ct, :], Q(0), ct % 4 == 0, False, ["CT", QK(0)], [PSK[yb]])
                        mm(ps[yb][:, 0:LS], CTw[:, 2, ct, :], Q(1), False, False, ["CT", QK(1)], [PSK[yb]])
                        mm(ps[yb][:, 0:LS], CTw[:, 1, ct, :], Q(2), False, False, ["CT", QK(2)], [PSK[yb]])
                        mm(ps[yb][:, 0:LS], CTw[:, 1, ct, :], Q(3), False, ct % 4 == 3, ["CT", QK(3)], [PSK[yb]])
                        if ct % 4 == 3:
                            stt(ysb[:], uTf[:, yt, cs], dskip[:, yt:yt + 1], ps[yb][:, 0:LS], ALU.mult, ALU.add,
                                ["uTf%d" % yt, "dskip", PSK[yb]], ["ysb"])
                            act(yg[:, yt, cs], ysb[:], AF.Gelu_apprx_tanh, ["ysb"], ["yg%d" % yt])
                            act(ygb[:, yt, cs], yg[:, yt, cs], AF.Copy, ["yg%d" % yt], ["ygb%d" % yt])
                    wks = ["wl%d" % i for i in range(16)]
                    EcL = Ec[:, :, LS - 1]
                    EsL = Es[:, :, LS - 1]
                    tt("dve", wt[:, 0, :], wl[:, 0, :], EcL, ALU.mult, wks + ["Ec"], ["wt0"])
                    tt("dve", wt[:, 1, :], wl[:, 1, :], EsL, ALU.mult, wks + ["Es"], ["wt1"])
                    tt("dve", wt[:, 2, :], wl[:, 1, :], EcL, ALU.mult, wks + ["Ec"], ["wt2"])
                    tt("dve", wt[:, 3, :], wl[:, 0, :], EsL, ALU.mult, wks + ["Es"], ["wt3"])
                    tt("dve", stc[:, 0, :], wt[:, 0, :], wt[:, 1, :], ALU.subtract, ["wt0", "wt1"], ["stc"])
                    tt("dve", stc[:, 1, :], wt[:, 2, :], wt[:, 3, :], ALU.add, ["wt2", "wt3"], ["stc"])
                for ot in range(4):
                    b = ot % 2
                    for kt in range(4):
                        mm(ps[b][:, :], Wglu[:, kt, ot * 128:(ot + 1) * 128], ygb[:, kt, :], kt == 0, kt == 3, ["Wglu", "ygb%d" % kt], [PSK[b]])
                    act(sg[:], ps[b][:, :], AF.Sigmoid, [PSK[b]], ["sg"])
                    tt("dve", yssm[:, ot, :], yg[:, ot, :], sg[:], ALU.mult, ["yg%d" % ot, "sg"], ["yssm%d" % ot])
                if debug:
                    dma("sp", d_y[:, t0:t0 + CT].rearrange("(o p) t -> p o t", p=128), yssm[:], ["yssm%d" % i for i in range(4)], [])
                for ot in range(8):
                    b = ot % 2
                    for kt in range(4):
                        mm(ps[b][:, :], Wbs[:, kt, ot * 128:(ot + 1) * 128], yssm[:, kt, :], kt == 0, kt == 3, ["Wbs", "yssm%d" % kt], [PSK[b]])
                    tt("dve", mixst[:, ot, :], ps[b][:, :], m1T[:, ot, :], ALU.mult, [PSK[b], "m1T"], ["mixst"])
                dma("pool", MIXA_d[:, t0:t0 + CT].rearrange("(o p) t -> p o t", p=128), mixst[:], ["mixst"], [])
        P.barrier()

        if stop_after == "A2":
            if debug:
                dm = dout("d_MIXA", [D, T], BF16)
                dma("sp", dm[:, :], MIXA_d[:, :], [], [])
            P.finish()
            return nc, dbg

        with ExitStack() as es:
            kwT = SB(es, "kwT", [128, 2, T], BF16)
            Vw = SB(es, "Vw", [128, 32, 2, 66], BF16)
            ms("dve", kwT[64:128], 0.0, ["kwT"])
            ms("dve", Vw[:, :, :, 64:66], 1.0, ["Vw"])
            for g in range(2):
                dma("sp", kwT[0:64, g, :], KWIN_d[g, :, :], [], ["kwT"])
                dma("sp", Vw[:, :, g, 0:64], VWIN_d.rearrange("(k p) c -> p k c", p=128)[:, :, g * 64:(g + 1) * 64], [], ["Vw"])
            kcT = SB(es, "kcT", [128, 2, 256], BF16)
            VCX = SB(es, "VCX", [128, 2, 2, 130], BF16)
            cmk = SB(es, "cmk", [128, 17, 4, 128], BF16)
            vA = SB(es, "vA", [128, 128])
            vB = SB(es, "vB", [128, 128])
            mdg = SB(es, "mdg", [128, 2, 4, 128], BF16)
            dma("pool", cmk[:], cmaskc[:, :, :, :], [], ["cmk"])
            dma("sp", vA[:], vmA[:, :], [], ["vA"])
            dma("sp", vB[:], vmB[:, :], [], ["vB"])
            dma("pool", mdg[:], mdiag[:, :, :, :], [], ["mdg"])
            ms("dve", VCX[:], 0.0, ["VCX"])
            ms("dve", VCX[:, :, :, 64:65], 1.0, ["VCX"])
            dma("pool", VCX[:, :, :, 65:129], ovx[:, :, :, :], [], ["VCX"])
            ms("dve", kcT[:], 0.0, ["kcT"])
            with ExitStack() as esc:
                kcm = SB(esc, "kcm", [128, 2, T], BF16)
                W1 = SB(esc, "W1", [128, 2, 32, 128], BF16)
                W2 = SB(esc, "W2", [128, 2, 64], BF16)
                peT = SB(esc, "peT", [128, 2, 32], BF16)
                hb = SB(esc, "hb", [128, 2])
                hg = SB(esc, "hg", [128, 256], BF16)
                dma("sp", kcm[:, 0, :], KCMP_d[:, :], [], ["kcm"])
                dma("sp", kcm[:, 1, :], VCMP_d[:, :], [], ["kcm"])
                for kv in range(2):
                    dma("pool", W1[:, kv, :, :], (wk1 if kv == 0 else wv1)[:, :, :], [], ["W1"])
                    dma("pool", peT[:, kv, :], (pekT if kv == 0 else pevT)[:, :], [], ["peT"])
                    dma("pool", W2[:, kv, :], (wk2 if kv == 0 else wv2)[:, :], [], ["W2"])
                for kv in range(2):
                    for j in range(32):
                        mm(ps[2][:, kv:kv + 1], W1[0:64, kv, j, :], peT[0:64, kv, j:j + 1], j == 0, j == 31, ["W1", "peT"], [PSK[2]])
                    cp("dve", hb[:, kv:kv + 1], ps[2][:, kv:kv + 1], [PSK[2]], ["hb"])
                for kv in range(2):
                    for g in range(2):
                        b = g
                        for j in range(32):
                            mm(ps[b][:, 0:255], W1[g * 64:(g + 1) * 64, kv, j, :], kcm[g * 64:(g + 1) * 64, kv, j:j + 16 * 254 + 1:16],
                               j == 0, j == 31, ["W1", "kcm"], [PSK[b]])
                        act(hg[:, 0:255], ps[b][:, 0:255], AF.Gelu_apprx_tanh, [PSK[b], "hb"], ["hg"], bias=hb[:, kv:kv + 1])
                        if kv == 0:
                            mm(ps[3][0:64, 0:255], W2[:, 0, :], hg[:, 0:255], True, True, ["W2", "hg"], [PSK[3]])
                            cp("dve", kcT[0:64, g, 0:255], ps[3][0:64, 0:255], [PSK[3]], ["kcT"])
                        else:
                            for nt in range(2):
                                rows = 128 if nt == 0 else 127
                                mm(ps[3][0:rows, nt * 64:(nt + 1) * 64], hg[:, nt * 128:nt * 128 + rows], W2[:, 1, :], True, True, ["W2", "hg"], [PSK[3]])
                                cp("dve", VCX[0:rows, nt, g, 0:64], ps[3][0:rows, nt * 64:(nt + 1) * 64], [PSK[3]], ["VCX"])
            P.barrier()
            if debug:
                d_kc = dout("d_kcT", [128, 2, 256], BF16)
                d_vcx = dout("d_VCX", [128, 2, 2, 130], BF16)
                dma("sp", d_kc[:, :, :], kcT[:], ["kcT"], [])
                dma("sp", d_vcx[:, :, :, :], VCX[:], ["VCX"], [])
                d_yn = dout("d_ynsa", [512, T], BF16)
                d_sel = dout("d_nsel", [T, 2, 64], BF16)
            QS = SB(es, "QS", [128, 3, 2, 4, 128], BF16)
            pT = SB(es, "pT", [128, 4, 512], BF16)
            O = SB(es, "O", [128, 2, 8, 64])
            Ob = SB(es, "Ob", [128, 512], BF16)
            ynT = SB(es, "ynT", [128, 4, CT], BF16)
            m2c = SB(es, "m2c", [128, 8, CT], BF16)
            mxa = SB(es, "mxa", [128, 8, CT], BF16)
            mxt = SB(es, "mxt", [128, CT])
            mixT = SB(es, "mixT", [128, 8, CT], BF16)
            xs = SB(es, "xs", [128, 4, D])
            hst = SB(es, "hst", [128, 4, D])
            Wbn = SB(es, "Wbn", [128, 4, D], BF16)
            Wout = SB(es, "Wout", [128, 8, D], BF16)
            sm = SB(es, "sm", [128, 96])
            imp = SB(es, "imp", [128, 64])
            scr = SB(es, "scr", [128, 64])
            wkk = SB(es, "wkk", [128, 64])
            nsb = SB(es, "nsb", [128, 2, 128], BF16)
            later = []
            dma("pool", Wbn[:], w_bn.rearrange("(k p) n -> p k n", p=128), [], ["Wbn"])
            dma("pool", Wout[:], w_out.rearrange("(k p) n -> p k n", p=128), [], ["Wout"])
            ms("dve", nsb[:], 0.0, ["nsb0", "nsb1"])
            ms("dve", QS[64:128], 0.0, ["QSn%d_%d" % (a_, b_) for a_ in range(3) for b_ in range(2)])
            state = {"sb": 0, "pt": 0}
            pendq = []
            SBK = (0, 1, 7)

            def s_tile(lhsT, rhs, rows, rk, mask, after):
                sb = SBK[state["sb"]]; state["sb"] = (state["sb"] + 1) % 3
                pi = state["pt"]; state["pt"] = (pi + 1) % 4
                mm(ps[sb][0:rows, :], lhsT, rhs, True, mask is None, rk, [PSK[sb]])
                if mask is not None:
                    mk, mkey = mask
                    mm(ps[sb][0:rows, :], idb[0:rows, 0:rows], mk, False, True, ["idb", mkey], [PSK[sb]])
                act(pT[0:rows, pi, :], ps[sb][0:rows, :], AF.Exp, [PSK[sb]], ["pT%d" % pi])
                pendq.append(lambda: after(pi))
                while len(pendq) > 2:
                    pendq.pop(0)()

            def pend_add(fn):
                if not pendq:
                    pendq.append(fn)
                    return
                prev = pendq[-1]

                def both():
                    prev()
                    fn()
                pendq[-1] = both

            def flush():
                while pendq:
                    pendq.pop(0)()

            def load_q(qb):
                buf = qb % 3
                dma("sp", QS[0:64, buf].rearrange("p g r t -> p (g r) t"), QT_d[:, :, qb * 128:(qb + 1) * 128].rearrange("h d t -> d h t"),
                    [], ["QSq%d" % buf])

            def emit_cmp(qb, g):
                buf = qb % 3
                obuf = qb % 2
                nts = [nt for nt in (0, 1) if qb - 16 * nt >= 0]
                gq = gates[:, qb, :]
                nbi = g

                def post():
                    ts("dve", sm[:, 0:2], ps[4][:, 64:64 + 130:129], 1e-30, ALU.add, [PSK[4]], ["sm_rs"])
                    ts("dve", sm[:, 2:4], ps[5][:, 64:64 + 130:129], 1e-30, ALU.add, [PSK[5]], ["sm_rs"])
                    P.op("dve", lambda e: e.reciprocal(sm[:, 0:4], sm[:, 0:4]), ["sm_rs"], ["sm_rs"])
                    for r in range(4):
                        bk = 4 + r // 2
                        o = (r % 2) * 129 + 65
                        if r == 0:
                            ts("dve", imp[:], ps[bk][:, o:o + 64], sm[:, 0:1], ALU.mult, [PSK[bk], "sm_rs"], ["imp"])
                        else:
                            stt(imp[:], ps[bk][:, o:o + 64], sm[:, r:r + 1], imp[:], ALU.mult, ALU.add, [PSK[bk], "sm_rs", "imp"], ["imp"])
                    tt("dve", sm[:, 4:8], sm[:, 0:4], gq[:, g * 4:g * 4 + 4], ALU.mult, ["sm_rs", "gates"], ["sm_cf"])
                    for r in range(4):
                        bk = 4 + r // 2
                        o = (r % 2) * 129
                        ts("dve", O[:, obuf, g * 4 + r, :], ps[bk][:, o:o + 64], sm[:, 4 + r:5 + r], ALU.mult, [PSK[bk], "sm_cf"], ["O%d_%d" % (obuf, g)])
                    c0 = 62 - 2 * qb
                    tt("dve", scr[:], imp[:], vA[:, c0:c0 + 64], ALU.mult, ["imp", "vA"], ["scr"])
                    tt("dve", scr[:], scr[:], vB[:, c0:c0 + 64], ALU.add, ["scr", "vB"], ["scr"])
                    ts("dve", scr[:, 0:1], scr[:, 0:1], 100.0, ALU.add, ["scr"], ["scr"])
                    P.op("dve", lambda e: e.max(out=sm[:, 32:40], in_=scr[:]), ["scr"], ["sm_m8"])
                    P.op("dve", lambda e: e.match_replace(out=wkk[:], in_to_replace=sm[:, 32:40], in_values=scr[:], imm_value=-1e30), ["scr", "sm_m8"], ["wkk"])
                    P.op("dve", lambda e: e.max(out=sm[:, 40:48], in_=wkk[:]), ["wkk"], ["sm_m8b"])
                    ts("dve", nsb[:, nbi, 64:128], scr[:], sm[:, 47:48], ALU.is_lt, ["scr", "sm_m8b"], ["nsb%d" % nbi], s2=NEGM, op1=ALU.mult)
                    if debug:
                        dma("sp", d_sel[qb * 128:(qb + 1) * 128, g, :], nsb[:, nbi, 64:128], ["nsb%d" % nbi], [])

                def post_pe():
                    pv = ps[6][:, 0:64].bitcast(BF16)
                    tr(pv, nsb[:, nbi, :], idb[:], ["nsb%d" % nbi, "idb"], [PSK[6]])
                    cp("dve", QS[64:128, buf, g], pv[64:128, :].unsqueeze(1).to_broadcast([64, 4, 128]), [PSK[6]], ["QSn%d_%d" % (buf, g)])
                later.append(post_pe)

                for nt in nts:
                    rows = 128 if nt == 0 else 127
                    dlt = qb - 16 * nt
                    mask = (cmk[0:rows, dlt].rearrange("p r t -> p (r t)"), "cmk") if dlt <= 16 else None

                    def after(pi, nt=nt, rows=rows):
                        for r in range(4):
                            bk = 4 + r // 2
                            o = (r % 2) * 129
                            mm(ps[bk][:, o:o + 129], pT[0:rows, pi, r * 128:(r + 1) * 128], VCX[0:rows, nt, g, 0:129],
                               nt == nts[0] and r % 2 == 0, (nt == nts[-1]) and r % 2 == 1, ["pT%d" % pi, "VCX"], [PSK[bk]])
                        if nt == nts[-1]:
                            post()
                    s_tile(kcT[:, g, nt * 128:nt * 128 + rows], QS[:, buf, g].rearrange("p r t -> p (r t)"), rows,
                           ["kcT", "QSq%d" % buf], mask, after)

            def finalize(bank, brn, qb, g):
                buf = qb % 2
                gq = gates[:, qb, :]
                k1, k2 = "sm_rs%d" % brn, "sm_cf%d" % brn
                o1 = 8 * brn
                ts("dve", sm[:, o1:o1 + 4], ps[bank][:, 64:64 + 65 * 3 + 1:65], 1e-30, ALU.add, [PSK[bank]], [k1])
                P.op("dve", lambda e: e.reciprocal(sm[:, o1:o1 + 4], sm[:, o1:o1 + 4]), [k1], [k1])
                tt("dve", sm[:, o1 + 4:o1 + 8], sm[:, o1:o1 + 4], gq[:, brn * 8 + g * 4:brn * 8 + g * 4 + 4], ALU.mult, [k1, "gates"], [k2])
                for r in range(4):
                    ok = "O%d_%d" % (buf, g)
                    stt(O[:, buf, g * 4 + r, :], ps[bank][:, r * 65:r * 65 + 64], sm[:, o1 + 4 + r:o1 + 5 + r], O[:, buf, g * 4 + r, :],
                        ALU.mult, ALU.add, [PSK[bank], k2, ok], [ok])

            def emit_selwin(qb, g):
                buf = qb % 3
                qk = ["QSq%d" % buf, "QSn%d_%d" % (buf, g)]
                rhs_full = QS[:, buf, g].rearrange("p r t -> p (r t)")
                rhs_q = QS[0:64, buf, g].rearrange("p r t -> p (r t)")
                for kt in range(qb + 1):
                    mask = (mdg[:, 0].rearrange("p r t -> p (r t)"), "mdg") if kt == qb else None

                    def after(pi, kt=kt):
                        for r in range(4):
                            mm(ps[2][:, r * 65:(r + 1) * 65], pT[:, pi, r * 128:(r + 1) * 128], Vs[:, kt, g, 0:65],
                               kt == 0 and r == 0, kt == qb and r == 3, ["pT%d" % pi, "Vs"], [PSK[2]])
                        if kt == qb:
                            finalize(2, 1, qb, g)
                    s_tile(KE[:, g, kt * 128:(kt + 1) * 128], rhs_full, 128, ["KE"] + qk, mask, after)
                k0 = max(0, qb - 4)
                for kt in range(k0, qb + 1):
                    mask = None
                    if kt == qb:
                        mask = (mdg[:, 0].rearrange("p r t -> p (r t)"), "mdg")
                    elif kt == qb - 4:
                        mask = (mdg[:, 1].rearrange("p r t -> p (r t)"), "mdg")

                    def after(pi, kt=kt):
                        for r in range(4):
                            mm(ps[3][:, r * 65:(r + 1) * 65], pT[:, pi, r * 128:(r + 1) * 128], Vw[:, kt, g, 0:65],
                               kt == k0 and r == 0, kt == qb and r == 3, ["pT%d" % pi, "Vw"], [PSK[3]])
                        if kt == qb:
                            finalize(3, 2, qb, g)
                    s_tile(kwT[:, g, kt * 128:(kt + 1) * 128], rhs_full, 128, ["kwT", "QSq%d" % buf], mask, after)

            def finish_qblock(qb):
                buf = qb % 2
                cp("pool", Ob[:], O[:, buf].rearrange("p h d -> p (h d)"), ["O%d_0" % buf, "O%d_1" % buf], ["Ob"])
                pv = ps[6][:, 0:256].bitcast(BF16)
                for f in range(4):
                    tr(pv[:, f * 128:(f + 1) * 128], Ob[:, f * 128:(f + 1) * 128], idb[:], ["Ob", "idb"], [PSK[6]])
                q4 = qb % 4
                cp("dve", ynT[:, :, q4 * 128:(q4 + 1) * 128], pv.rearrange("p (f t) -> p f t", f=4), [PSK[6]], ["ynT"])

            def project(c):
                t0 = c * CT
                dma("sp", m2c[:], M2_d[:, t0:t0 + CT].rearrange("(o p) t -> p o t", p=128), [], ["m2c"])
                dma("sp", mxa[:], MIXA_d[:, t0:t0 + CT].rearrange("(o p) t -> p o t", p=128), [], ["mxa"])
                dma("sp", xs[:], x[t0:t0 + CT, :].rearrange("(s p) d -> p s d", p=128), [], ["xs"])
                if debug:
                    dma("sp", d_yn[:, t0:t0 + CT].rearrange("(o p) t -> p o t", p=128), ynT[:], ["ynT"], [])
                for ot in range(8):
                    b = 6
                    for kt in range(4):
                        mm(ps[b][:, :], Wbn[:, kt, ot * 128:(ot + 1) * 128], ynT[:, kt, :], kt == 0, kt == 3, ["Wbn", "ynT"], [PSK[b]])
                    tt("dve", mxt[:], ps[b][:, :], m2c[:, ot, :], ALU.mult, [PSK[b], "m2c"], ["mxt"])
                    tt("dve", mixT[:, ot, :], mxt[:], mxa[:, ot, :], ALU.add, ["mxt", "mxa"], ["mixT%d" % ot])
                for s in range(4):
                    for nh in range(2):
                        b = 6
                        for kt in range(8):
                            mm(ps[b][:, :], mixT[:, kt, s * 128:(s + 1) * 128], Wout[:, kt, nh * 512:(nh + 1) * 512], kt == 0, kt == 7,
                               ["Wout", "mixT%d" % kt], [PSK[b]])
                        tt("dve", hst[:, s, nh * 512:(nh + 1) * 512], ps[b][:, :], xs[:, s, nh * 512:(nh + 1) * 512], ALU.add, [PSK[b], "xs"], ["hst"])
                dma("pool", H_d[t0:t0 + CT, :].rearrange("(s p) d -> p s d", p=128), hst[:], ["hst"], [])

            NI = 64
            load_q(0)
            for i in range(NI + 1):
                if i < NI:
                    qb, g = i // 2, i % 2
                    if g == 0 and qb + 1 < 32:
                        load_q(qb + 1)
                    emit_cmp(qb, g)
                if i >= 1:
                    qb, g = (i - 1) // 2, (i - 1) % 2
                    emit_selwin(qb, g)
                    if g == 1:
                        def fin(qb=qb):
                            finish_qblock(qb)
                            if qb % 4 == 3:
                                project(qb // 4)
                        pend_add(fin)
                if i == 0:
                    flush()
                while later:
                    later.pop(0)()
            flush()
        P.barrier()
        esKV.close()
        if stop_after == "B":
            if debug:
                dh = dout("d_H", [T, D])
                dma("sp", dh[:, :], H_d[:, :], [], [])
            P.finish()
            return nc, dbg

        def norm_T(gb, src, dstb, nT, ssq, gkey, tag=""):
            for s_ in range(4):
                act(dstb[:, s_, :], src[:, s_, :], AF.Square, [gkey], ["nb%s%d" % (tag, s_), "ssq"], accum_out=ssq[:, s_:s_ + 1])
            act(ssq[:, 4:8], ssq[:, 0:4], AF.Sqrt, ["ssq"], ["ssq2"], scale=1.0 / D, bias=EPS)
            P.op("dve", lambda e: e.reciprocal(ssq[:, 4:8], ssq[:, 4:8]), ["ssq2"], ["ssq2"])
            for s_ in range(4):
                stt(dstb[:, s_, :], src[:, s_, :], ssq[:, 4 + s_:5 + s_], gb[:], ALU.mult, ALU.mult, [gkey, "ssq2", "gb"], ["nb%s%d" % (tag, s_)])
            for kt in range(8):
                b = 2 + (kt % 2)
                pv = ps[b][:, 0:256].bitcast(BF16)
                for s_ in range(4):
                    tr(pv[:, s_ * 128:(s_ + 1) * 128], dstb[:, s_, kt * 128:(kt + 1) * 128], idb[:], ["nb%s%d" % (tag, s_), "idb"], [PSK[b]])
                if kt % 2 == 0:
                    act(nT[:, kt, :], pv, AF.Copy, [PSK[b]], ["nT%s%d" % (tag, kt)])
                else:
                    cp("dve", nT[:, kt, :], pv, [PSK[b]], ["nT%s%d" % (tag, kt)])

        with ExitStack() as es:
            Wup = SB(es, "Wup", [128, 8, 2 * DFF], BF16)
            gf = SB(es, "gf", [128, D])
            cw = SB(es, "cw", [128, 44, 3])
            cb = SB(es, "cb", [128, 44])
            dma("sp", gf[:], gffn[:, :], [], ["gb"])
            dma("sp", cw[:], cvw[:, :, :], [], ["cw"])
            dma("sp", cb[:], cvb[:, :], [], ["cw"])
            for q4 in range(4):
                for hf in range(2):
                    c0 = hf * DFF + q4 * 704
                    dma("pool", Wup[:, :, c0:c0 + 704], w_up.rearrange("(k p) n -> p k n", p=128)[:, :, c0:c0 + 704], [], ["Wup%d" % q4])
            hs = SB(es, "hs", [128, 4, D])
            hb_ = SB(es, "hb_", [128, 2, 4, D], BF16)
            ssq = SB(es, "ssq", [128, 8])
            n2T = SB(es, "n2T", [128, 2, 8, CT], BF16)
            he = SB(es, "he", [128, 3, CT + 2])
            acc = SB(es, "acc", [128, 3, CT])
            ga = SB(es, "ga", [128, CT])
            carry = SB(es, "carry", [128, 44, 2])
            actT = SB(es, "actT", [128, 22, CT], BF16)
            ms("dve", carry[:], 0.0, ["carry%d" % i for i in range(44)])
            def c1_pro(c):
                t0_ = c * CT
                dma("sp", hs[:], H_d[t0_:t0_ + CT, :].rearrange("(s p) d -> p s d", p=128), [], ["hs"])
                norm_T(gf, hs, hb_[:, c % 2], n2T[:, c % 2], ssq, "hs", tag="%d_" % (c % 2))

            c1_pro(0)
            for c in range(NCH):
                t0 = c * CT
                pbc = c % 2
                order = []
                for i in range(22):
                    order += [i, 22 + i]
                tail = [None]
                for oi, ft in enumerate(order):
                    b = (0, 1, 4, 5)[oi % 4]
                    par = oi % 3
                    wk_ = "Wup%d" % (((ft % 22) * 128) // 704)
                    wk2_ = "Wup%d" % (((ft % 22) * 128 + 127) // 704)
                    if oi == 20 and c + 1 < NCH:
                        c1_pro(c + 1)
                    for kt in range(8):
                        mm(ps[b][:, :], Wup[:, kt, ft * 128:(ft + 1) * 128], n2T[:, pbc, kt, :], kt == 0, kt == 7, [wk_, wk2_, "nT%d_%d" % (pbc, kt)], [PSK[b]])
                    hk, hck, ak = "he%d" % par, "hec%d" % par, "acc%d" % par
                    cp("dve", he[:, par, 0:2], carry[:, ft, :], ["carry%d" % ft], [hck])
                    act(he[:, par, 2:CT + 2], ps[b][:, :], AF.Copy, [PSK[b]], [hk])
                    act(acc[:, par, :], ps[b][:, :], AF.Identity, [PSK[b], "cw"], [ak], scale=cw[:, ft, 2:3], bias=cb[:, ft:ft + 1])
                    stt(acc[:, par, :], he[:, par, 1:CT + 1], cw[:, ft, 1:2], acc[:, par, :], ALU.mult, ALU.add, [hk, hck, ak, "cw"], [ak])
                    stt(acc[:, par, :], he[:, par, 0:CT], cw[:, ft, 0:1], acc[:, par, :], ALU.mult, ALU.add, [hk, hck, ak, "cw"], [ak])
                    cp("dve", carry[:, ft, :], he[:, par, CT:CT + 2], [hk], ["carry%d" % ft])
                    prev = tail[0]

                    def tl(ft=ft, par=par, ak=ak):
                        if ft < 22:
                            act(ga[:], acc[:, par, :], AF.Gelu_apprx_tanh, [ak], ["ga"])
                        else:
                            tt("dve", actT[:, ft - 22, :], ga[:], acc[:, par, :], ALU.mult, ["ga", ak], ["actT"])
                    tail[0] = tl
                    if prev is not None:
                        prev()
                tail[0]()
                dma("pool", ACT_d[:, t0:t0 + CT].rearrange("(o p) t -> p o t", p=128), actT[:], ["actT"], [])
        P.barrier()
        if stop_after == "C1":
            P.finish()
            return nc, dbg

        with ExitStack() as es:
            Wdn = SB(es, "Wdn", [128, 22, D], BF16)
            Wpg = SB(es, "Wpg", [128, 8, D], BF16)
            Wpp = SB(es, "Wpp", [128, 2, D], BF16)
            gfb = SB(es, "gfb", [128, D])
            gp = SB(es, "gp", [128, D])
            dma("sp", gp[:], gple[:, :], [], ["gb"])
            dma("sp", gfb[:], gfin[:, :], [], ["gfb"])
            dma("pool", Wdn[:], w_down.rearrange("(k p) n -> p k n", p=128), [], ["Wdn"])
            dma("pool", Wpp[:], w_pp.rearrange("(k p) n -> p k n", p=128), [], ["Wpp"])
            dma("pool", Wpg[:], w_pg.rearrange("(k p) n -> p k n", p=128), [], ["Wpg"])
            aT = SB(es, "aT", [128, 2, 22, CT], BF16)
            hs2 = SB(es, "hs2", [128, 2, 4, D])
            h2b = SB(es, "h2b", [128, 4, D], BF16)
            ssq = SB(es, "ssq", [128, 12])
            n3T = SB(es, "n3T", [128, 8, CT], BF16)
            pin = SB(es, "pin", [128, 2, 4, 256])
            pb = SB(es, "pb", [128, 4, 256], BF16)
            ppT = SB(es, "ppT", [128, 2, CT], BF16)
            gsb = SB(es, "gsb", [128, CT])
            tmp = SB(es, "tmp", [128, CT])
            junk = SB(es, "junk", [128, D], BF16)

            def wdown(c):
                t0 = c * CT
                q = c % 2
                hk = "hs%d" % q
                dma("sp", aT[:, q], ACT_d[:, t0:t0 + CT].rearrange("(o p) t -> p o t", p=128), [], ["aT%d" % q])
                dma("sp", hs2[:, q], H_d[t0:t0 + CT, :].rearrange("(s p) d -> p s d", p=128), [], [hk])
                dma("sp", pin[:, q], pp[t0:t0 + CT, :].rearrange("(s p) d -> p s d", p=128), [], ["pin%d" % q])
                for s_ in range(4):
                    for nh in range(2):
                        b = 6 + (s_ * 2 + nh) % 2
                        for kt in range(22):
                            mm(ps[b][:, :], aT[:, q, kt, s_ * 128:(s_ + 1) * 128], Wdn[:, kt, nh * 512:(nh + 1) * 512], kt == 0, kt == 21,
                               ["aT%d" % q, "Wdn"], [PSK[b]])
                        tt("dve", hs2[:, q, s_, nh * 512:(nh + 1) * 512], ps[b][:, :], hs2[:, q, s_, nh * 512:(nh + 1) * 512], ALU.add, [PSK[b], hk], [hk])

            def norm_a(src, dstb, gkey):
                for s_ in range(4):
                    act(dstb[:, s_, :], src[:, s_, :], AF.Square, [gkey], ["nb%d" % s_, "ssq"], accum_out=ssq[:, s_:s_ + 1])
                act(ssq[:, 4:8], ssq[:, 0:4], AF.Sqrt, ["ssq"], ["ssq2"], scale=1.0 / D, bias=EPS)
                P.op("dve", lambda e: e.reciprocal(ssq[:, 4:8], ssq[:, 4:8]), ["ssq2"], ["ssq2"])
                for s_ in range(4):
                    stt(dstb[:, s_, :], src[:, s_, :], ssq[:, 4 + s_:5 + s_], gp[:], ALU.mult, ALU.mult, [gkey, "ssq2", "gb"], ["nb%d" % s_])

            def norm_b(src, dstb, nT):
                for kt in range(8):
                    b = 2 + (kt % 2)
                    pv = ps[b][:, 0:256].bitcast(BF16)
                    for s_ in range(4):
                        tr(pv[:, s_ * 128:(s_ + 1) * 128], dstb[:, s_, kt * 128:(kt + 1) * 128], idb[:], ["nb%d" % s_, "idb"], [PSK[b]])
                    if kt % 2 == 0:
                        act(nT[:, kt, :], pv, AF.Copy, [PSK[b]], ["nT%d" % kt])
                    else:
                        cp("dve", nT[:, kt, :], pv, [PSK[b]], ["nT%d" % kt])

            def rest(c):
                t0 = c * CT
                q = c % 2
                hk = "hs%d" % q
                hs = hs2[:, q]
                norm_b(hs, h2b, n3T)
                cp("pool", pb[:], pin[:, q], ["pin%d" % q], ["pb"])
                for kt in range(2):
                    pv = ps[2][:, 0:256].bitcast(BF16)
                    for s_ in range(4):
                        tr(pv[:, s_ * 128:(s_ + 1) * 128], pb[:, s_, kt * 128:(kt + 1) * 128], idb[:], ["pb", "idb"], [PSK[2]])
                    cp("dve", ppT[:, kt, :], pv, [PSK[2]], ["ppT"])
                for s_ in range(4):
                    for nh in range(2):
                        b = (s_ * 2 + nh) % 2
                        for kt in range(8):
                            mm(ps[b][:, :], n3T[:, kt, s_ * 128:(s_ + 1) * 128], Wpg[:, kt, nh * 512:(nh + 1) * 512], kt == 0, kt == 7,
                               ["nT%d" % kt, "Wpg"], [PSK[b]])
                        act(gsb[:], ps[b][:, :], AF.Sigmoid, [PSK[b]], ["gsb"])
                        b2 = 4 + b
                        for kt in range(2):
                            mm(ps[b2][:, :], ppT[:, kt, s_ * 128:(s_ + 1) * 128], Wpp[:, kt, nh * 512:(nh + 1) * 512], kt == 0, kt == 1,
                               ["ppT", "Wpp"], [PSK[b2]])
                        tt("dve", tmp[:], ps[b2][:, :], gsb[:], ALU.mult, [PSK[b2], "gsb"], ["tmp"])
                        tt("dve", hs[:, s_, nh * 512:(nh + 1) * 512], hs[:, s_, nh * 512:(nh + 1) * 512], tmp[:], ALU.add, ["tmp", hk], [hk])
                for s_ in range(4):
                    act(junk[:], hs[:, s_, :], AF.Square, [hk], ["junk", "ssf"], accum_out=ssq[:, 8 + s_:9 + s_])
                act(ssq[:, 8:12], ssq[:, 8:12], AF.Sqrt, ["ssf"], ["ssf"], scale=1.0 / D, bias=EPS)
                P.op("dve", lambda e: e.reciprocal(ssq[:, 8:12], ssq[:, 8:12]), ["ssf"], ["ssf"])
                for s_ in range(4):
                    stt(hs[:, s_, :], hs[:, s_, :], ssq[:, 8 + s_:9 + s_], gfb[:], ALU.mult, ALU.mult, [hk, "ssf", "gfb"], [hk])
                dma("pool", out[t0:t0 + CT, :].rearrange("(s p) d -> p s d", p=128), hs, [hk], [])

            wdown(0)
            for c in range(NCH):
                norm_a(hs2[:, c % 2], h2b, "hs%d" % (c % 2))
                if c + 1 < NCH:
                    wdown(c + 1)
                rest(c)
        P.finish()
    return nc, dbg


def host_prep(inp, b):
    f = np.float32
    m = {}
    m["x"] = np.ascontiguousarray(inp["x"][b])
    m["p"] = np.ascontiguousarray(inp["p"][0, b])
    m["w_in"] = np.ascontiguousarray(inp["w_in"][0])
    for nm, k in (("gmix", "g_mix"), ("gffn", "g_ffn"), ("gple", "g_ple")):
        m[nm] = np.ascontiguousarray(np.broadcast_to(inp[k][0][None, :], (128, D)))
    m["gfin"] = np.ascontiguousarray(np.broadcast_to(inp["g_final"][None, :], (128, D)))

    def chl(a):
        return np.ascontiguousarray(a.reshape(16, 2, 64).transpose(1, 2, 0).reshape(128, 16))
    m["are"] = chl(inp["ssm_a_re"][0])
    m["aim"] = chl(inp["ssm_a_im"][0])
    m["ldt"] = chl(np.broadcast_to(inp["ssm_log_dt"][0][:, None], (32, 64)))
    bn_re = np.zeros((128, 16, 128), f); bn_im = np.zeros((128, 16, 128), f)
    cn_re = np.zeros((128, 16, 128), f); cn_im = np.zeros((128, 16, 128), f)
    for g in range(32):
        ct, gl = g // 2, g % 2
        o = (g % 8) * 16
        bn_re[gl * 64:(gl + 1) * 64, ct, o:o + 16] = inp["ssm_b_re"][0, g]
        bn_im[gl * 64:(gl + 1) * 64, ct, o:o + 16] = inp["ssm_b_im"][0, g]
        cn_re[o:o + 16, ct, gl * 64:(gl + 1) * 64] = inp["ssm_c_re"][0, g]
        cn_im[o:o + 16, ct, gl * 64:(gl + 1) * 64] = inp["ssm_c_im"][0, g]
    m["bnre"], m["bnim"], m["cnre"], m["cnim"] = bn_re, bn_im, cn_re, cn_im
    m["dsk"] = np.ascontiguousarray(inp["ssm_d"][0].reshape(4, 128).T)
    m["w_glu"] = np.ascontiguousarray(inp["ssm_w_glu"][0])
    m["w_bs"] = np.ascontiguousarray(inp["w_br_ssm"][0])
    m["w_bn"] = np.ascontiguousarray(inp["w_br_nsa"][0])
    m["w_out"] = np.ascontiguousarray(inp["w_out"][0])
    m["w_up"] = np.ascontiguousarray(inp["w_up"][0])
    m["cvw"] = np.ascontiguousarray(inp["conv_w"][0].T.reshape(44, 128, 3).transpose(1, 0, 2))
    m["cvb"] = np.ascontiguousarray(inp["conv_b"][0].reshape(44, 128).T)
    m["w_down"] = np.ascontiguousarray(inp["w_down"][0])
    m["w_pg"] = np.ascontiguousarray(inp["w_ple_gate"][0])
    m["w_pp"] = np.ascontiguousarray(inp["w_ple_proj"][0])
    def w1l(a):
        t_ = a.reshape(32, 64, 128).transpose(1, 0, 2)
        return np.ascontiguousarray(np.concatenate([t_, t_], axis=0))
    m["wk1"] = w1l(inp["cmp_wk1"][0]); m["wv1"] = w1l(inp["cmp_wv1"][0])
    m["wk2"] = np.ascontiguousarray(inp["cmp_wk2"][0]); m["wv2"] = np.ascontiguousarray(inp["cmp_wv2"][0])
    m["pekT"] = np.ascontiguousarray(np.concatenate([inp["cmp_pe_k"][0].T] * 2, axis=0))
    m["pevT"] = np.ascontiguousarray(np.concatenate([inp["cmp_pe_v"][0].T] * 2, axis=0))
    m["ident"] = np.eye(128, dtype=f)
    ek = np.zeros((64, T), f); ek[np.arange(T) // 64, np.arange(T)] = 1.0
    m["ekey"] = ek
    nl = np.arange(128)[:, None]; tl = np.arange(128)[None, :]
    cm = np.zeros((128, 17, 128), f)
    for dlt in range(17):
        cm[:, dlt, :] = np.where(16 * nl + 31 <= 128 * dlt + tl, 0.0, NEGM)
    m["cmaskc"] = np.ascontiguousarray(np.broadcast_to(cm[:, :, None, :], (128, 17, 4, 128)))
    tlp = np.arange(128)[:, None]; mm_ = np.arange(128)[None, :]
    valid = ((mm_ - 62) <= tlp // 64).astype(f)
    forced = (((mm_ - 62) == tlp // 64) | ((mm_ - 62) == tlp // 64 - 1)).astype(f)
    m["vmA"] = valid
    m["vmB"] = (valid - 1.0) + 100.0 * forced
    kk = np.arange(128)[:, None]; qq = np.arange(128)[None, :]
    md_ = np.stack([np.where(kk <= qq, 0.0, NEGM), np.where(kk > qq, 0.0, NEGM)], axis=1).astype(f)
    m["mdiag"] = np.ascontiguousarray(np.broadcast_to(md_[:, :, None, :], (128, 2, 4, 128)))
    starts = np.arange(256) * 16; ss_ = np.arange(64) * 64
    ov = ((starts[:, None] < ss_[None, :] + 64) & (starts[:, None] + 32 > ss_[None, :])).astype(f)
    ov[255] = 0
    ovl = ov.reshape(2, 128, 64).transpose(1, 0, 2)
    m["ovx"] = np.ascontiguousarray(np.broadcast_to(ovl[:, :, None, :], (128, 2, 2, 64)))
    return m


_CACHE = {}


def kernel(**inputs):
    inp = {k: np.asarray(v) for k, v in inputs.items()}
    if "nc" not in _CACHE:
        _CACHE["nc"] = build("C", False)[0]
    nc = _CACHE["nc"]
    in_maps = [host_prep(inp, b) for b in range(8)]
    res = run_bass_kernel_spmd(nc, in_maps, core_ids=list(range(8)))
    return np.stack([res.results[b]["out"] for b in range(8)], axis=0).astype(np.float32)
```
